# Optimizing a Trainium2 kernel written in Bass

```python
import math
import jax, jax.numpy as jnp
from jax import lax
import numpy as np

D_MODEL = 1024
BATCH = 32
SEQ = 256
DEPTH = 4
DEC_BATCH = 2
DEC_SEQ = 1024
PAST_LEN = 512

GRID_W = 64
N_MIXERS = 2
N_SSD = (DEPTH + 1) // 2
N_FNO = DEPTH // 2
EXPAND = 2
D_INNER = EXPAND * D_MODEL
HEAD_DIM = 64
N_HEADS = D_INNER // HEAD_DIM
N_GROUPS = 4
D_STATE = 128
CONV_W = 3
CHUNK = 128
D_XBC = D_INNER + 2 * N_GROUPS * D_STATE
D_IN_PROJ = D_INNER + D_XBC + 2 * N_HEADS
N_FGROUPS = 4
D_FF = 4 * D_MODEL
EPS = 1e-6

kernel_name = "bidir_ssd_fnet_hybrid_diffusion_step"


def rmsnorm(x, w):
    xf = x.astype(jnp.float32)
    xf = xf * lax.rsqrt(jnp.mean(xf * xf, axis=-1, keepdims=True) + EPS)
    return xf.astype(x.dtype) * w


def ada_modulation(cond, w, b):
    m = jax.nn.silu(cond) @ w + b
    return jnp.split(m[:, None, :], 6, axis=-1)


def centred_dwconv(x, w, bias, n_seg):
    b, L, C = x.shape
    seg = L // n_seg
    xs = x.reshape(b * n_seg, seg, C)
    pad = CONV_W // 2
    xp = jnp.pad(xs, ((0, 0), (pad, pad), (0, 0)))
    out = sum(xp[:, k:k + seg] * w[k] for k in range(CONV_W)) + bias
    return out.reshape(b, L, C)


def ssd_scan(x, dt, A, Bm, Cm, h0):
    b, L, H, P = x.shape
    G, N = Bm.shape[2], Bm.shape[3]
    R = H // G
    nc = L // CHUNK
    f32 = jnp.float32
    xc = x.astype(f32).reshape(b, nc, CHUNK, G, R, P)
    dtc = dt.astype(f32).reshape(b, nc, CHUNK, G, R)
    Bc = Bm.astype(f32).reshape(b, nc, CHUNK, G, N)
    Cc = Cm.astype(f32).reshape(b, nc, CHUNK, G, N)
    dA = dtc * A.astype(f32).reshape(G, R)
    cs = jnp.cumsum(dA, axis=2)
    seg = cs[:, :, :, None] - cs[:, :, None, :]
    mask = jnp.tril(jnp.ones((CHUNK, CHUNK), dtype=bool))[:, :, None, None]
    decay = jnp.exp(jnp.where(mask, seg, -jnp.inf))
    CB = jnp.einsum("bclgn,bcsgn->bclsg", Cc, Bc)
    y_diag = jnp.einsum("bclsg,bclsgr,bcsgr,bcsgrp->bclgrp", CB, decay, dtc, xc)
    decay_to_end = jnp.exp(cs[:, :, -1:] - cs)
    states = jnp.einsum("bclgn,bclgr,bclgrp->bcgrpn", Bc, decay_to_end * dtc, xc)
    chunk_decay = jnp.exp(cs[:, :, -1])

    def step(h, inp):
        s, d = inp
        return h * d[..., None, None] + s, h

    h_final, h_starts = lax.scan(step, h0.astype(f32).reshape(b, G, R, P, N),
                                 (jnp.moveaxis(states, 1, 0), jnp.moveaxis(chunk_decay, 1, 0)))
    h_starts = jnp.moveaxis(h_starts, 0, 1)
    y_off = jnp.einsum("bclgn,bcgrpn,bclgr->bclgrp", Cc, h_starts, jnp.exp(cs))
    y = (y_diag + y_off).reshape(b, L, H, P)
    return y.astype(x.dtype), h_final.reshape(b, H, P, N)


def ssd_mixer(h, n_seg, h0_f, h0_b, w_in, conv_w, conv_b, dt_bias, a_log, d_skip, norm_w, w_out):
    b, L, _ = h.shape
    proj = h @ w_in
    z = proj[..., :D_INNER]
    xbc = proj[..., D_INNER:D_INNER + D_XBC]
    dt_raw = proj[..., D_INNER + D_XBC:].reshape(b, L, 2, N_HEADS)
    xbc = jax.nn.silu(centred_dwconv(xbc, conv_w, conv_b, n_seg))
    xs = xbc[..., :D_INNER].reshape(b, L, N_HEADS, HEAD_DIM)
    Bm = xbc[..., D_INNER:D_INNER + N_GROUPS * D_STATE].reshape(b, L, N_GROUPS, D_STATE)
    Cm = xbc[..., D_INNER + N_GROUPS * D_STATE:].reshape(b, L, N_GROUPS, D_STATE)
    dt = jax.nn.softplus((dt_raw + dt_bias).astype(jnp.float32))
    A = -jnp.exp(a_log.astype(jnp.float32))
    y_f, hf = ssd_scan(xs, dt[:, :, 0], A[0], Bm, Cm, h0_f)
    y_b, hb = ssd_scan(jnp.flip(xs, 1), jnp.flip(dt[:, :, 1], 1), A[1],
                       jnp.flip(Bm, 1), jnp.flip(Cm, 1), h0_b)
    y = y_f + jnp.flip(y_b, 1) + d_skip[:, None] * xs
    y = y.reshape(b, L, D_INNER) * jax.nn.silu(z)
    return rmsnorm(y, norm_w) @ w_out, hf, hb


def fourier_mixer(h, w_out, b_out):
    b, L, D = h.shape
    hg = h.astype(jnp.float32).reshape(b, L, N_FGROUPS, D // N_FGROUPS)
    f = jnp.fft.fft2(hg, axes=(1, 3), norm="ortho").real
    return f.reshape(b, L, D).astype(h.dtype) @ w_out + b_out


def sq_relu_mlp(h, w1, w2):
    return jnp.square(jax.nn.relu(h @ w1)) @ w2


def setup_inputs(seed: int = 0) -> dict:
    key = jax.random.key(seed)
    ks = jax.random.split(key, 24)
    nrm = jax.random.normal
    dt0 = jnp.exp(jax.random.uniform(ks[0], (N_SSD, 2, N_HEADS),
                                     minval=math.log(1e-3), maxval=math.log(1e-1)))
    dt_bias = dt0 + jnp.log(-jnp.expm1(-dt0))
    a_log = jnp.log(jax.random.uniform(ks[1], (N_SSD, 2, N_HEADS), minval=1.0, maxval=16.0))
    return {
        "x_prompt": nrm(ks[2], (BATCH, SEQ, D_MODEL), jnp.float32),
        "x_sample": nrm(ks[3], (DEC_BATCH, DEC_SEQ, D_MODEL), jnp.float32),
        "state_ssd": 0.5 * nrm(ks[4], (DEC_BATCH, N_SSD, 2, N_HEADS, HEAD_DIM, D_STATE), jnp.float32),
        "c": nrm(ks[5], (DEC_BATCH, D_MODEL), jnp.float32),
        "c_ctx": nrm(ks[6], (D_MODEL,), jnp.float32),
        "ada_w": 0.5 * D_MODEL ** -0.5 * nrm(ks[7], (DEPTH, D_MODEL, 6 * D_MODEL), jnp.float32),
        "ada_b": 0.02 * nrm(ks[8], (DEPTH, 6 * D_MODEL), jnp.float32),
        "norm_mix_w": 1.0 + 0.02 * nrm(ks[9], (DEPTH, D_MODEL), jnp.float32),
        "norm_mlp_w": 1.0 + 0.02 * nrm(ks[10], (DEPTH, D_MODEL), jnp.float32),
        "ssd_w_in": D_MODEL ** -0.5 * nrm(ks[11], (N_SSD, D_MODEL, D_IN_PROJ), jnp.float32),
        "ssd_conv_w": CONV_W ** -0.5 * nrm(ks[12], (N_SSD, CONV_W, D_XBC), jnp.float32),
        "ssd_conv_b": 0.02 * nrm(ks[13], (N_SSD, D_XBC), jnp.float32),
        "ssd_dt_bias": dt_bias,
        "ssd_a_log": a_log,
        "ssd_d": 1.0 + 0.1 * nrm(ks[14], (N_SSD, N_HEADS), jnp.float32),
        "ssd_norm_w": 1.0 + 0.02 * nrm(ks[15], (N_SSD, D_INNER), jnp.float32),
        "ssd_w_out": D_INNER ** -0.5 * nrm(ks[16], (N_SSD, D_INNER, D_MODEL), jnp.float32),
        "fno_w_out": D_MODEL ** -0.5 * nrm(ks[17], (N_FNO, D_MODEL, D_MODEL), jnp.float32),
        "fno_b_out": 0.02 * nrm(ks[18], (N_FNO, D_MODEL), jnp.float32),
        "mlp_w1": D_MODEL ** -0.5 * nrm(ks[19], (DEPTH, D_MODEL, D_FF), jnp.float32),
        "mlp_w2": D_FF ** -0.5 * nrm(ks[20], (DEPTH, D_FF, D_MODEL), jnp.float32),
        "final_norm_w": 1.0 + 0.02 * nrm(ks[21], (D_MODEL,), jnp.float32),
    }


def reference(x_prompt, x_sample, state_ssd, c, c_ctx, ada_w, ada_b, norm_mix_w, norm_mlp_w,
              ssd_w_in, ssd_conv_w, ssd_conv_b, ssd_dt_bias, ssd_a_log, ssd_d, ssd_norm_w,
              ssd_w_out, fno_w_out, fno_b_out, mlp_w1, mlp_w2, final_norm_w):
    rows = x_sample.shape[1] // GRID_W
    b_ctx = x_prompt.shape[0]
    xp, xs = x_prompt, x_sample
    zeros_state = jnp.zeros((b_ctx, N_HEADS, HEAD_DIM, D_STATE), jnp.float32)
    new_states = []
    for i in range(DEPTH):
        sm_p, cm_p, gm_p, sf_p, cf_p, gf_p = ada_modulation(c_ctx[None], ada_w[i], ada_b[i])
        sm_s, cm_s, gm_s, sf_s, cf_s, gf_s = ada_modulation(c, ada_w[i], ada_b[i])
        hp = rmsnorm(xp, norm_mix_w[i]) * (1.0 + cm_p) + sm_p
        hs = rmsnorm(xs, norm_mix_w[i]) * (1.0 + cm_s) + sm_s
        if i % N_MIXERS == 0:
            j = i // N_MIXERS
            prm = (ssd_w_in[j], ssd_conv_w[j], ssd_conv_b[j], ssd_dt_bias[j], ssd_a_log[j],
                   ssd_d[j], ssd_norm_w[j], ssd_w_out[j])
            op, hf, hb = ssd_mixer(hp, 1, zeros_state, zeros_state, *prm)
            new_states.append(jnp.stack([hf, hb], axis=1).astype(x_prompt.dtype))
            os_, _, _ = ssd_mixer(hs, rows, state_ssd[:, j, 0], state_ssd[:, j, 1], *prm)
        else:
            j = i // N_MIXERS
            op = fourier_mixer(hp, fno_w_out[j], fno_b_out[j])
            os_ = fourier_mixer(hs, fno_w_out[j], fno_b_out[j])
        xp = xp + gm_p * op
        xs = xs + gm_s * os_
        hp = rmsnorm(xp, norm_mlp_w[i]) * (1.0 + cf_p) + sf_p
        hs = rmsnorm(xs, norm_mlp_w[i]) * (1.0 + cf_s) + sf_s
        xp = xp + gf_p * sq_relu_mlp(hp, mlp_w1[i], mlp_w2[i])
        xs = xs + gf_s * sq_relu_mlp(hs, mlp_w1[i], mlp_w2[i])
    new_state_ssd = jnp.stack(new_states, axis=1)
    y_prompt = rmsnorm(xp, final_norm_w)
    y_sample = rmsnorm(xs, final_norm_w)
    return (y_prompt, y_sample, new_state_ssd)
```

```python
import contextlib
import numpy as np
import ml_dtypes
import concourse.bass as bass
import concourse.mybir as mybir
from concourse.bass_utils import run_bass_kernel_spmd

F32 = mybir.dt.float32
BF16 = mybir.dt.bfloat16
AF = mybir.ActivationFunctionType
ALU = mybir.AluOpType

ENGS = ("pe", "act", "dve", "pool", "sp")
T = 1280
NCH = 10
EPS = 1e-6
NSLOT = 5
PRIO_CRITICAL_PATH = False
SCHED_WINDOW = 1500
FILL_THR = 0.25
FILL_DUR = 0.25
FILL_MAX = 6
SLOT_ELEMS = 2048
RANGES = ((0, 512), (512, 1024), (1024, 1280))
URANGES = ((0, 0, 1024), (1, 1024, 1280))


class Prog:
    def __init__(self, nc, dry=False):
        self.nc = nc
        self.dry = dry
        self.ops = []
        self.last_w = {}
        self.readers = {}
        self.chan_last = {}
        self.chans = []
        self.phase = 0
        self.fill = None
        self.n_fill = 0

    def barrier(self):
        self.phase += 1

    def _add(self, eng, fn, reads, writes, chan=None, cost=0.5):
        if self.dry:
            return -1
        idx = len(self.ops)
        deps = set()
        odeps = set()
        for k in reads:
            w = self.last_w.get(k)
            if w is not None:
                deps.add(w)
        for k in writes:
            w = self.last_w.get(k)
            if w is not None:
                deps.add(w)
            for r in self.readers.get(k, ()):
                deps.add(r)
        if chan is not None:
            if chan not in self.chan_last:
                self.chans.append(chan)
            p = self.chan_last.get(chan)
            if p is not None:
                deps.add(p)
            self.chan_last[chan] = idx
        for k in writes:
            self.last_w[k] = idx
            self.readers[k] = []
        for k in reads:
            if k not in writes:
                lst = self.readers.setdefault(k, [])
                if chan is None:
                    for q, r in enumerate(lst):
                        if self.ops[r]["chan"] is None and self.ops[r]["eng"] == eng:
                            odeps.add(r)
                            lst[q] = idx
                            break
                    else:
                        lst.append(idx)
                else:
                    lst.append(idx)
        deps.discard(idx)
        odeps.discard(idx)
        odeps -= deps
        self.ops.append(dict(eng=eng, fn=fn, deps=sorted(deps), odeps=sorted(odeps), chan=chan, cost=cost,
                             phase=self.phase, fill=(self.fill if (eng == "pe" and chan is None) else None), gap=0.0))
        return idx

    def op(self, eng, fn, reads=(), writes=(), cost=0.5):
        return self._add(eng, fn, tuple(reads), tuple(writes), cost=cost)

    def dma(self, q, out, in_, reads=(), writes=(), chan="d0"):
        nbytes = 4.0
        for d in in_.shape:
            nbytes *= d
        return self._add(q, lambda e: e.dma_start(out=out, in_=in_), tuple(reads), tuple(writes), chan=chan,
                         cost=1.5 + nbytes / 300e3)

    def schedule(self, reorder=True):
        import heapq
        ops = self.ops
        n = len(ops)
        dependents = [[] for _ in range(n)]
        for i, o in enumerate(ops):
            for d in o["deps"]:
                dependents[d].append(i)
            for d in o["odeps"]:
                dependents[d].append(i)
        by_phase = {}
        for i, o in enumerate(ops):
            by_phase.setdefault(o["phase"], []).append(i)
        LAT = 0.1
        bl = [0.0] * n
        for i in range(n - 1, -1, -1):
            m = 0.0
            for j in dependents[i]:
                if bl[j] + LAT > m:
                    m = bl[j] + LAT
            bl[i] = ops[i]["cost"] + m
        if not PRIO_CRITICAL_PATH:
            bl = [0.0] * n
        t_e = {e: 0.0 for e in ENGS}
        fin_t = [None] * n
        order = []
        dma_free = 0.0
        last_sched_eng = {}
        last_sched_chan = {}
        for ph in sorted(by_phase):
            idxs = by_phase[ph]
            B = set(last_sched_eng.values())
            for ch, idx in last_sched_chan.items():
                if not (isinstance(ch, tuple) and ch[0] == "w"):
                    B.add(idx)
            nrem = {}
            ready_t = {}
            pending = {e: [] for e in ENGS}
            avail = {e: [] for e in ENGS}
            for i in idxs:
                o = ops[i]
                if B:
                    o["deps"] = sorted(set(o["deps"]) | B)
                r = 0.0
                cnt = 0
                for d in list(o["deps"]) + list(o["odeps"]):
                    if fin_t[d] is None:
                        cnt += 1
                    else:
                        r = max(r, fin_t[d] + LAT)
                nrem[i] = cnt
                ready_t[i] = r
                if cnt == 0:
                    heapq.heappush(pending[o["eng"]], (r if reorder else 0.0, -bl[i] if reorder else 0.0, i))
            done = 0
            sched_flag = {i: False for i in idxs}
            lo_pos = 0
            while done < len(idxs):
                best = None
                while lo_pos < len(idxs) and sched_flag[idxs[lo_pos]]:
                    lo_pos += 1
                lo_idx = idxs[lo_pos] if lo_pos < len(idxs) else 0
                for use_window in (True, False):
                  for e in ENGS:
                    pe_, av = pending[e], avail[e]
                    while pe_ and pe_[0][0] <= t_e[e]:
                        q_ = heapq.heappop(pe_)
                        heapq.heappush(av, (q_[1], q_[2]))
                    if av and use_window and av[0][1] > lo_idx + SCHED_WINDOW:
                        if pe_ and pe_[0][2] <= lo_idx + SCHED_WINDOW:
                            cand = (pe_[0][0], pe_[0][1], pe_[0][2], e, False)
                        else:
                            continue
                    elif av:
                        cand = (t_e[e], av[0][0], av[0][1], e, True)
                    elif pe_:
                        cand = (pe_[0][0], pe_[0][1], pe_[0][2], e, False)
                    else:
                        continue
                    key = cand[:3] if reorder else (cand[2],)
                    if best is None or key < best[0]:
                        best = (key, cand)
                  if best is not None:
                      break
                start, _pri, i, e, from_av = best[1]
                sched_flag[i] = True
                if from_av:
                    heapq.heappop(avail[e])
                else:
                    heapq.heappop(pending[e])
                o = ops[i]
                if e == "pe":
                    o["gap"] = max(0.0, start - t_e[e])
                start = max(start, t_e[e])
                if o["chan"] is not None:
                    t_e[e] = start + (1.0 if e == "pool" else 0.1)
                    fin = max(start, dma_free) + o["cost"]
                    dma_free = fin - 1.5
                    last_sched_chan[o["chan"]] = i
                else:
                    t_e[e] = start + o["cost"]
                    fin = t_e[e]
                    last_sched_eng[e] = i
                fin_t[i] = fin
                order.append(i)
                done += 1
                for j in dependents[i]:
                    if j in nrem:
                        if fin + LAT > ready_t[j]:
                            ready_t[j] = fin + LAT
                        nrem[j] -= 1
                        if nrem[j] == 0:
                            heapq.heappush(pending[ops[j]["eng"]], (ready_t[j] if reorder else 0.0, -bl[j] if reorder else 0.0, j))
        assert len(order) == n
        self.est_us = max(t_e.values())
        newidx = {old: new for new, old in enumerate(order)}
        new_ops = []
        for old in order:
            o = ops[old]
            o["deps"] = sorted(newidx[d] for d in o["deps"])
            o["odeps"] = sorted(newidx[d] for d in o["odeps"])
            assert all(d < newidx[old] for d in o["deps"]) and all(d < newidx[old] for d in o["odeps"])
            new_ops.append(o)
        self.ops = new_ops

    def emit(self, reorder=True):
        nc = self.nc
        self.schedule(reorder)
        ops = self.ops
        n = len(ops)

        def pe_pe(od, o):
            return od["chan"] is None and od["eng"] == "pe" and o["eng"] == "pe" and o["chan"] is None

        needed = [False] * n
        for i, o in enumerate(ops):
            for d in o["deps"]:
                if pe_pe(ops[d], o):
                    continue
                needed[d] = True
        last_of = {}
        for i, o in enumerate(ops):
            if o["chan"] is None:
                last_of[o["eng"]] = i
        for e, i in last_of.items():
            needed[i] = True
        cnt = {e: 0 for e in ENGS}
        ccnt = {c: 0 for c in self.chans}
        sig = [None] * n
        for i, o in enumerate(ops):
            if o["chan"] is not None:
                ccnt[o["chan"]] += 16
                sig[i] = (("c", o["chan"]), ccnt[o["chan"]])
            elif needed[i]:
                cnt[o["eng"]] += 1
                sig[i] = (("e", o["eng"]), cnt[o["eng"]])
        known = {e: {} for e in ENGS}
        waits = [None] * n
        for i, o in enumerate(ops):
            e = o["eng"]
            w = {}
            for d in o["deps"]:
                if pe_pe(ops[d], o):
                    continue
                sk, v = sig[d]
                if known[e].get(sk, 0) >= v:
                    continue
                if w.get(sk, 0) < v:
                    w[sk] = v
            for sk, v in w.items():
                known[e][sk] = v
            waits[i] = sorted(w.items(), key=lambda t: str(t[0]))
        final = {}
        for i, o in enumerate(ops):
            if sig[i] is not None:
                sk, v = sig[i]
                if final.get(sk, 0) < v:
                    final[sk] = v
        self.stats = dict(n_ops=n, per_eng={e: sum(1 for o in ops if o["eng"] == e) for e in ENGS},
                          n_sig=sum(1 for s in sig if s is not None), n_waits=sum(len(w) for w in waits),
                          maxcnt=dict(cnt))
        with contextlib.ExitStack() as st:
            sems = {}
            for e in ENGS:
                sems[("e", e)] = st.enter_context(nc.semaphore("s_" + e))
            for ci, c in enumerate(self.chans):
                sems[("c", c)] = st.enter_context(nc.semaphore("c_%d" % ci))
            block = st.enter_context(nc.Block())

            def run(engname):
                def body(eng):
                    for i, o in enumerate(ops):
                        if o["eng"] != engname:
                            continue
                        if o["fill"] is not None and o["gap"] > FILL_THR:
                            fo, fl, fr = o["fill"]
                            for _ in range(min(FILL_MAX, int(o["gap"] / FILL_DUR))):
                                eng.matmul(fo, fl, fr, start=True, stop=True, skip_group_check=True)
                                self.n_fill += 1
                        for sk, v in waits[i]:
                            eng.wait_ge(sems[sk], v)
                        ins = o["fn"](eng)
                        if sig[i] is not None:
                            sk, v = sig[i]
                            ins.then_inc(sems[sk], 16 if sk[0] == "c" else 1)
                    if engname == "sp":
                        for sk, v in sorted(final.items(), key=lambda t: str(t[0])):
                            eng.wait_ge(sems[sk], v)
                return body

            block.tensor(run("pe"))
            block.scalar(run("act"))
            block.vector(run("dve"))
            block.gpsimd(run("pool"))
            block.sync(run("sp"))


class WStream:
    def __init__(self, P, slots, plan=None):
        self.P = P
        self.slots = slots
        self.plan = plan
        self.rec = []
        self.n = 0
        self.issued = 0

    def view(self, i, kt, ncols):
        return self.slots[:, i % NSLOT, 0:kt * ncols].rearrange("p (k n) -> p k n", k=kt)

    def get(self, dram_ap, kt, ncols):
        i = self.n
        self.n += 1
        self.rec.append((dram_ap, kt, ncols))
        if self.plan is not None:
            while self.issued < min(len(self.plan), i + NSLOT):
                q = self.issued
                ap, k2, n2 = self.plan[q]
                self.P.dma("pool", self.view(q, k2, n2), ap.rearrange("(k p) n -> p k n", p=128),
                           writes=[("slot", q % NSLOT)], chan=("w", q % NSLOT))
                self.issued += 1
        return self.view(i, kt, ncols), ("slot", i % NSLOT)


SM_LAYER = 64
SM_SSD0 = 4 * SM_LAYER
SM_FNO0 = SM_SSD0 + 2 * 128
SM_FIN = SM_FNO0 + 16
NSM = SM_FIN + 8
RW_SSD = 160
RW_KF = 2 * RW_SSD
RW_CUT = RW_KF + 20
NR = RW_CUT + 20
CB_ID, CB_ONES, CB_TRIF, CB_TRIB, CB_NEGM, CB_ECOL, CB_ZERO = 0, 128, 256, 384, 512, 1536, 1600
NCB = 1728


def build_program(n_layers=4, dbg=False):
    nc = bass.Bass("TRN2", target_bir_lowering=False)
    dr = lambda name, shape, dt: nc.dram_tensor(name, shape, dt, kind="ExternalInput").ap()
    xT_d = dr("xT", [1024, T], F32)
    cond_d = dr("condT", [128, 16], F32)
    sm_d = dr("smalls", [128, NSM], F32)
    rows_d = dr("rows", [1, NR], F32)
    init_d = dr("init_st", [2, 2, 128, 2048], F32)
    cb_d = dr("cb", [128, NCB], BF16)
    dftA_d = dr("dftA", [1024, 2, 1024], BF16)
    dftB_d = dr("dftB", [256, 2, 256], BF16)
    csd_d = dr("csd", [256, 512], BF16)
    ada_w = dr("ada_w", [4, 1024, 6144], F32)
    w_in = dr("ssd_w_in", [2, 1024, 5184], F32)
    w_out = dr("ssd_w_out", [2, 2048, 1024], F32)
    fno_w = dr("fno_w_out", [2, 1024, 1024], F32)
    w1 = dr("mlp_w1", [4, 1024, 4096], F32)
    w2 = dr("mlp_w2", [4, 4096, 1024], F32)
    yT_d = nc.dram_tensor("yT", [1024, T], F32, kind="ExternalOutput").ap()
    st_d = nc.dram_tensor("st_out", [5, 2, 2, 128, 2048], F32, kind="ExternalOutput").ap()

    with contextlib.ExitStack() as st:
        ARENA = 209920
        arena = st.enter_context(nc.sbuf_tensor("arena", [128, ARENA // 2], BF16))
        PA = [st.enter_context(nc.psum_tensor("pa%d" % i, [128, 1536], F32)) for i in range(2)]
        PM = [st.enter_context(nc.psum_tensor("pm%d" % i, [128, 512], F32)) for i in range(2)]

        off = [0]

        def alloc(nbytes, dt, pat=None, **kw):
            o = off[0]
            nb = (nbytes + 63) // 64 * 64
            off[0] = o + nb
            assert off[0] <= ARENA, ("arena overflow", off[0])
            return view_at(o, nbytes, dt, pat, **kw)

        def view_at(o, nbytes, dt, pat=None, **kw):
            v = arena[:, o // 2:(o + nbytes) // 2]
            if dt == F32:
                v = v.bitcast(F32)
            if pat:
                v = v.rearrange(pat, **kw)
            return v

        x = alloc(8 * T * 4, F32, "p (k t) -> p k t", k=8)
        hT = alloc(8 * T * 2, BF16, "p (k t) -> p k t", k=8)
        slots = alloc(NSLOT * SLOT_ELEMS * 2, BF16, "p (s e) -> p s e", s=NSLOT)
        cb = alloc(NCB * 2, BF16)
        sm = alloc(NSM * 4, F32)
        rows = alloc(NR * 4, F32)
        condT = alloc(16 * 4, F32, "p (k u) -> p k u", u=2)
        scond = alloc(16 * 2, BF16, "p (k u) -> p k u", u=2)
        mods = [alloc(64 * 2 * 4, F32, "p (m u) -> p m u", u=2) for _ in range(2)]
        rstd = alloc(T * 4, F32)
        PR0 = off[0]
        PR_BYTES = ARENA - PR0

        ident = cb[:, CB_ID:CB_ID + 128]
        ones_b = cb[:, CB_ONES:CB_ONES + 128]
        tri = [cb[:, CB_TRIF:CB_TRIF + 128], cb[:, CB_TRIB:CB_TRIB + 128]]
        negm = [cb[:, CB_NEGM:CB_NEGM + 512], cb[:, CB_NEGM + 512:CB_NEGM + 1024]]
        ecol = cb[:, CB_ECOL:CB_ECOL + 64]
        zeros_b = cb[:, CB_ZERO:CB_ZERO + 128]

        def pbank(acc, b):
            return ("pa", acc, b)

        PAK = [[pbank(a, b) for b in range(3)] for a in range(2)]
        PMK = [("pm", 0), ("pm", 1)]

        def gen(P, ws):
            def fsz(ap):
                n_ = 1
                for d_ in ap.shape[1:]:
                    n_ *= d_
                return n_

            def mm(out, lhsT, rhs, start, stop, reads, writes):
                P.op("pe", lambda e: e.matmul(out, lhsT, rhs, start=start, stop=stop, skip_group_check=True), reads, writes,
                     cost=0.06 + fsz(out) / 2400.0)

            def tr(out, in_, reads, writes, start):
                P.op("pe", lambda e: e.matmul(out, in_, ident, start=start, stop=True, skip_group_check=True),
                     tuple(reads) + ("cb",), writes, cost=0.12)

            def act(out, in_, func, reads, writes, bias=None, scale=None):
                kw = {}
                if bias is not None:
                    kw["bias"] = bias
                if scale is not None:
                    kw["scale"] = scale
                P.op("act", lambda e: e.activation(out=out, in_=in_, func=func, **kw), reads, writes,
                     cost=0.22 + fsz(out) / 1200.0)

            def vcost(eng, out):
                return (0.12 + fsz(out) / 900.0) * (2.0 if eng == "pool" else 1.0)

            def tt(eng, out, in0, in1, op, reads, writes):
                P.op(eng, lambda e: e.tensor_tensor(out=out, in0=in0, in1=in1, op=op), reads, writes, cost=vcost(eng, out))

            def stt(out, in0, scalar, in1, op0, op1, reads, writes):
                P.op("dve", lambda e: e.scalar_tensor_tensor(out=out, in0=in0, scalar=scalar, in1=in1, op0=op0, op1=op1), reads, writes,
                     cost=vcost("dve", out))

            def cp(eng, out, in_, reads, writes):
                if eng == "act":
                    P.op("act", lambda e: e.activation(out=out, in_=in_, func=AF.Copy), reads, writes,
                         cost=0.22 + fsz(out) / 1200.0)
                else:
                    P.op(eng, lambda e: e.tensor_copy(out=out, in_=in_), reads, writes, cost=vcost(eng, out))

            def tsc(eng, out, in0, s1, op0, reads, writes):
                P.op(eng, lambda e: e.tensor_scalar(out=out, in0=in0, scalar1=s1, scalar2=None, op0=op0), reads, writes,
                     cost=vcost(eng, out))

            def memset(eng, out, val, writes):
                P.op(eng, lambda e: e.memset(out, val), (), writes, cost=vcost(eng, out))

            P.dma("sp", cb, cb_d, writes=["cb"], chan="ld0")
            P.dma("sp", sm, sm_d, writes=["sm"], chan="ld1")
            P.dma("sp", rows, rows_d.partition_broadcast(128), writes=["rows"], chan="ld2")
            P.dma("sp", condT.rearrange("p k u -> p (k u)"), cond_d, writes=["cond"], chan="ld3")
            for k in range(8):
                P.dma("sp", x[:, k, :], xT_d[k * 128:(k + 1) * 128, :], writes=[("x", k)], chan=("ldx", k % 4))
            act(scond, condT, AF.Silu, ["cond"], ["scond"])

            def ada_steps(i):
                mb = mods[i % 2]
                halves = []
                for half in range(2):
                    mk = ("modsA" if half == 0 else "modsB", i % 2)
                    steps = []
                    for pq in range(12):
                        def step(pq=pq, half=half, mk=mk):
                            pc = half * 12 + pq
                            wv, wk = ws.get(ada_w[i][:, pc * 256:(pc + 1) * 256], 8, 256)
                            for o in range(2):
                                ot = pc * 2 + o
                                for k in range(8):
                                    mm(PM[0][:, ot * 2:ot * 2 + 2], wv[:, k, o * 128:(o + 1) * 128], scond[:, k, :],
                                       start=(pq == 0 and o == 0 and k == 0), stop=(pq == 11 and o == 1 and k == 7),
                                       reads=[wk, "scond"], writes=[PMK[0]])
                            if pq == 11:
                                c0 = half * 24
                                adab = sm[:, i * SM_LAYER + c0:i * SM_LAYER + c0 + 24]
                                tt("dve", mb[:, c0:c0 + 24, :], PM[0][:, 2 * c0:2 * c0 + 48].rearrange("p (m u) -> p m u", u=2),
                                   adab.unsqueeze(2).to_broadcast([128, 24, 2]), ALU.add, [PMK[0], "sm"], [mk])
                                nw = sm[:, i * SM_LAYER + 48 + 8 * half:i * SM_LAYER + 56 + 8 * half]
                                stt(mb[:, 48 + 8 * half:56 + 8 * half, :], mb[:, 8 + 24 * half:16 + 24 * half, :], 1.0,
                                    nw.unsqueeze(2).to_broadcast([128, 8, 2]), ALU.add, ALU.mult, [mk, "sm"], [mk])
                        steps.append(step)
                    halves.append(steps)
                return halves

            def rms_stats(src_tiles, nk, sqbuf, inv_n, src_keys, sq_func_in_bf16=False):
                for k in range(nk):
                    sq = sqbuf[k % 2]
                    sqk = ("sq", k % 2)
                    act(sq, src_tiles(k), AF.Square, [src_keys(k)], [sqk])
                    for r, (r0, r1) in enumerate(RANGES):
                        mm(PA[0][:, r0:r1], ones_b, sq[:, r0:r1], start=(k == 0), stop=(k == nk - 1),
                           reads=[sqk, "cb"], writes=[PAK[0][r]])
                act(rstd, PA[0][:, 0:T], AF.Sqrt, PAK[0], ["rstd"], bias=EPS, scale=inv_n)
                P.op("dve", lambda e: e.reciprocal(out=rstd, in_=rstd), ["rstd"], ["rstd"], cost=1.5)

            def norm_mod(mb, mk, a_off, b_off, sqbuf, tmpf):
                rms_stats(lambda k: x[:, k, :], 8, sqbuf, 1.0 / 1024.0, lambda k: ("x", k))
                for k in range(8):
                    tf = tmpf[k % 2]
                    tk = ("tmpf", k % 2)
                    for u, r0, r1 in URANGES:
                        stt(tf[:, r0:r1], x[:, k, r0:r1], mb[:, a_off + k, u:u + 1], rstd[:, r0:r1], ALU.mult, ALU.mult,
                            [("x", k), mk, "rstd"], [tk])
                    for u, r0, r1 in URANGES:
                        act(hT[:, k, r0:r1], tf[:, r0:r1], AF.Identity, [tk, mk], [("hT", k)], bias=mb[:, b_off + k, u:u + 1])

            def resid_add(ot, src, src_keys, mb, mk, g_off):
                for u, r0, r1 in URANGES:
                    stt(x[:, ot, r0:r1], src[:, r0:r1], mb[:, g_off + ot, u:u + 1], x[:, ot, r0:r1], ALU.mult, ALU.add,
                        list(src_keys) + [mk, ("x", ot)], [("x", ot)])

            def lin_tile(acc, wv, wk, col0, in_tile, in_key, k_list, first, last):
                for ki, (kslot, kin) in enumerate(k_list):
                    for r, (r0, r1) in enumerate(RANGES):
                        mm(PA[acc][:, r0:r1], wv[:, kslot, col0:col0 + 128], in_tile(kin)[:, r0:r1],
                           start=(first and ki == 0), stop=(last and ki == len(k_list) - 1),
                           reads=[wk, in_key(kin)], writes=[PAK[acc][r]])

            def mlp(i, mb, mk, inter_steps):
                o = PR0
                h1T = view_at(o, 32 * T * 2, BF16, "p (k t) -> p k t", k=32)
                o += 32 * T * 2
                rt = [view_at(o + q * T * 4, T * 4, F32) for q in range(2)]
                sqbuf = [view_at(PR0 + q * T * 2, T * 2, BF16) for q in range(2)]
                P.barrier()
                norm_mod(mb, mk, 56, 24, sqbuf, rt)
                inter = list(inter_steps)
                for pc in range(16):
                    wv, wk = ws.get(w1[i][:, pc * 256:(pc + 1) * 256], 8, 256)
                    for oo in range(2):
                        ft = pc * 2 + oo
                        a = ft % 2
                        lin_tile(a, wv, wk, oo * 128, lambda k: hT[:, k, :], lambda k: ("hT", k),
                                 [(k, k) for k in range(8)], True, True)
                        act(rt[a], PA[a][:, 0:T], AF.Relu, PAK[a], [("rt", a)])
                        tt("dve", h1T[:, ft, :], rt[a], rt[a], ALU.mult, [("rt", a)], [("h1T", ft)])
                    if inter:
                        inter.pop(0)()
                for ot in range(8):
                    a = ot % 2
                    for half in range(2):
                        wv, wk = ws.get(w2[i][half * 2048:(half + 1) * 2048, ot * 128:(ot + 1) * 128], 16, 128)
                        lin_tile(a, wv, wk, 0, lambda k: h1T[:, k, :], lambda k: ("h1T", k),
                                 [(kk, half * 16 + kk) for kk in range(16)], half == 0, half == 1)
                        if inter:
                            inter.pop(0)()
                    resid_add(ot, PA[a], PAK[a], mb, mk, 40)
                while inter:
                    inter.pop(0)()

            def fnet(j, i, mb, mk):
                o = PR0
                Y = view_at(o, 10 * 4 * 512 * 2, BF16, "p (t g c) -> p t g c", t=10, g=4)
                sqbuf = [view_at(o + q * T * 2, T * 2, BF16) for q in range(2)]
                o += 10 * 4 * 512 * 2
                fT = view_at(o, 8 * T * 2, BF16, "p (k t) -> p k t", k=8)
                o += 8 * T * 2
                dA_s = view_at(o, 8 * 2 * 1024 * 2, BF16, "p (l c k) -> p l c k", l=8, c=2)
                o += 8 * 2 * 1024 * 2
                dB_s = view_at(o, 2 * 2 * 256 * 2, BF16, "p (l c k) -> p l c k", l=2, c=2)
                o += 2 * 2 * 256 * 2
                csd_s = view_at(o, 2 * 512 * 2, BF16, "p (l c) -> p l c", l=2)
                o += 2 * 512 * 2
                assert o <= ARENA, o
                tmpf = [view_at(PR0 + 2 * T * 2 + q * T * 4, T * 4, F32) for q in range(2)]
                P.barrier()
                for l in range(8):
                    P.dma("sp", dA_s[:, l], dftA_d[l * 128:(l + 1) * 128], writes=[("dftA", l)], chan=("ldf", l % 4))
                for l in range(2):
                    P.dma("sp", dB_s[:, l], dftB_d[l * 128:(l + 1) * 128], writes=["dftB"], chan=("ldf", l))
                    P.dma("sp", csd_s[:, l], csd_d[l * 128:(l + 1) * 128], writes=["csd"], chan=("ldf", 2 + l))
                norm_mod(mb, mk, 48, 0, sqbuf, tmpf)
                P.barrier()
                n = 0
                for tt_ in range(10):
                    for gc in range(4):
                        b = n % 2
                        for kt in range(2):
                            mm(PM[b][:, 0:512], hT[:, 2 * gc + kt, tt_ * 128:(tt_ + 1) * 128], csd_s[:, kt, :],
                               start=(kt == 0), stop=(kt == 1), reads=[("hT", 2 * gc + kt), "csd"], writes=[PMK[b]])
                        cp("act" if n % 2 == 0 else "dve", Y[:, tt_, gc, :], PM[b][:, 0:512], [PMK[b]], [("Y", tt_)])
                        n += 1
                for e_ in range(8):
                    gc, half = e_ // 2, e_ % 2
                    a = e_ % 2
                    for r in range(2):
                        cnt = 0
                        for lt in range(8):
                            for cs_ in range(2):
                                mm(PA[a][:, r * 512:(r + 1) * 512], Y[:, lt, gc, cs_ * 256 + half * 128:cs_ * 256 + half * 128 + 128],
                                   dA_s[:, lt, cs_, r * 512:(r + 1) * 512], start=(cnt == 0), stop=(cnt == 15),
                                   reads=[("Y", lt), ("dftA", lt)], writes=[PAK[a][r]])
                                cnt += 1
                    cnt = 0
                    for lt in range(8, 10):
                        for cs_ in range(2):
                            mm(PA[a][:, 1024:1280], Y[:, lt, gc, cs_ * 256 + half * 128:cs_ * 256 + half * 128 + 128],
                               dB_s[:, lt - 8, cs_, :], start=(cnt == 0), stop=(cnt == 3),
                               reads=[("Y", lt), "dftB"], writes=[PAK[a][2]])
                            cnt += 1
                    cp("act" if e_ % 2 == 0 else "dve", fT[:, e_, :], PA[a][:, 0:T], PAK[a], [("fT", e_)])
                P.barrier()
                for pc in range(4):
                    wv, wk = ws.get(fno_w[j][:, pc * 256:(pc + 1) * 256], 8, 256)
                    for oo in range(2):
                        ot = pc * 2 + oo
                        a = ot % 2
                        lin_tile(a, wv, wk, oo * 128, lambda k: fT[:, k, :], lambda k: ("fT", k),
                                 [(k, k) for k in range(8)], True, True)
                        act(tmpf[a], PA[a][:, 0:T], AF.Identity, PAK[a] + ["sm"], [("tmpf", a)],
                            bias=sm[:, SM_FNO0 + j * 8 + ot:SM_FNO0 + j * 8 + ot + 1])
                        resid_add(ot, tmpf[a], [("tmpf", a)], mb, mk, 16)

            def ssd(j, i, mb, mk, inter_z=()):
                smb = SM_SSD0 + j * 128
                convw = sm[:, smb:smb + 72].rearrange("p (t k) -> p t k", k=3)
                convb = sm[:, smb + 72:smb + 96]
                normw = sm[:, smb + 96:smb + 112]
                rwb = j * RW_SSD
                dtb_bc = rows[:, rwb:rwb + 64]
                alog_bc = rows[:, rwb + 64:rwb + 128]
                d_bc = rows[:, rwb + 128:rwb + 160]
                kf_bc = rows[:, RW_KF:RW_KF + 20].rearrange("p (d c) -> p d c", d=2)
                ncut_bc = rows[:, RW_CUT:RW_CUT + 19]
                wj = w_in[j]

                o = [PR0]

                def al(nbytes, dt, pat=None, **kw):
                    v = view_at(o[0], nbytes, dt, pat, **kw)
                    o[0] += (nbytes + 63) // 64 * 64
                    assert o[0] <= ARENA, ("ssd region overflow", o[0])
                    return v

                ygT = al(16 * T * 2, BF16, "p (k t) -> p k t", k=16)
                BgT = al(T * 2, BF16)
                CgT = al(T * 2, BF16)
                xTt = [al(T * 2, BF16) for _ in range(2)]
                x_tm = al(10 * 512 * 2, BF16, "p (c f) -> p c f", c=10)
                B_tm = al(10 * 128 * 2, BF16, "p (c f) -> p c f", c=10)
                CBT = al(10 * 128 * 2, BF16, "p (c f) -> p c f", c=10)
                hb_bf = al(10 * 512 * 2, BF16, "p (c f) -> p c f", c=10)
                hf_bf = al(512 * 2, BF16)
                LT2 = al(512 * 2, BF16)
                H = [al(512 * 4, F32) for _ in range(2)]
                LT = [al(512 * 2, BF16) for _ in range(2)]
                MT = [al(512 * 2, BF16) for _ in range(2)]
                yo = [al(512 * 2, BF16) for _ in range(2)]
                xwt = [al(512 * 2, BF16) for _ in range(2)]
                xD = al(512 * 2, BF16)
                y_tm2 = [al(512 * 2, BF16) for _ in range(2)]
                cv1 = al(T * 4, F32)
                e_offk = al(640 * 4, F32, "p (d c h) -> p d c h", d=2, c=10)
                wst = al(640 * 4, F32, "p (d c h) -> p d c h", d=2, c=10)
                kcd = al(640 * 4, F32, "p (d c h) -> p d c h", d=2, c=10)
                cshlT = al(T * 2, BF16)
                nuhlT = al(T * 2, BF16)
                corr = al(2 * 2 * 20 * 4, F32, "p (q a b) -> p q a b", q=2, a=2)
                hb_off = PR0 + 16 * T * 2 + 4 * ((T * 2 + 63) // 64 * 64) + 10 * 512 * 2 + 2 * 10 * 128 * 2
                xraw = view_at(hb_off, T * 4, F32)
                cacc = view_at(hb_off + T * 4, T * 4, F32)
                so = [PR0 + 16 * T * 2 + 4 * ((T * 2 + 63) // 64 * 64)]

                def sal(nbytes, dt, pat=None, **kw):
                    v = view_at(so[0], nbytes, dt, pat, **kw)
                    so[0] += (nbytes + 63) // 64 * 64
                    return v

                dtv = sal(640 * 4, F32, "p (c j) -> p c j", c=10)
                dA = sal(640 * 4, F32, "p (c j) -> p c j", c=10)
                dAh = sal(640 * 2, BF16, "p (c j) -> p c j", c=10)
                dAl = sal(640 * 2, BF16, "p (c j) -> p c j", c=10)
                A_bc = sal(64 * 4, F32)
                cs = sal(640 * 4, F32, "p (d c h) -> p d c h", d=2, c=10)
                tot = sal(640 * 4, F32, "p (d c h) -> p d c h", d=2, c=10)
                t1 = sal(640 * 4, F32, "p (d c h) -> p d c h", d=2, c=10)
                negu = sal(640 * 4, F32, "p (d c h) -> p d c h", d=2, c=10)
                cshl = sal(10 * 128 * 2, BF16, "p (c f) -> p c f", c=10)
                nuhl = sal(10 * 128 * 2, BF16, "p (c f) -> p c f", c=10)
                assert so[0] <= PR0 + 16 * T * 2 + 4 * ((T * 2 + 63) // 64 * 64) + 10 * 512 * 2 * 2 + 2 * 10 * 128 * 2 + 1024, so[0]
                dt_dch = dtv.rearrange("p c (d h) -> p d c h", d=2)

                sqbuf = [xraw.bitcast(BF16)[:, 0:T], cacc.bitcast(BF16)[:, 0:T]]
                P.barrier()
                norm_mod(mb, mk, 48, 0, sqbuf, [xraw, cacc])
                P.barrier()

                wv, wk = ws.get(wj[:, 5120:5184], 8, 64)
                for c in range(NCH):
                    bnk = 0 if c < 8 else 1
                    cc = c if c < 8 else c - 8
                    for k in range(8):
                        mm(PM[bnk][:, cc * 64:(cc + 1) * 64], hT[:, k, c * 128:(c + 1) * 128], wv[:, k, 0:64],
                           start=(cc == 0 and k == 0), stop=(k == 7), reads=[wk, ("hT", k)], writes=[PMK[bnk]])
                tt("dve", dtv[:, 0:8, :], PM[0][:, 0:512].rearrange("p (c j) -> p c j", c=8),
                   dtb_bc.unsqueeze(1).to_broadcast([128, 8, 64]), ALU.add, [PMK[0], "rows"], ["dtv"])
                tt("dve", dtv[:, 8:10, :], PM[1][:, 0:128].rearrange("p (c j) -> p c j", c=2),
                   dtb_bc.unsqueeze(1).to_broadcast([128, 2, 64]), ALU.add, [PMK[1], "rows"], ["dtv"])
                act(dtv, dtv, AF.Exp, ["dtv"], ["dtv"])
                act(dtv, dtv, AF.Ln, ["dtv"], ["dtv"], bias=1.0)
                act(A_bc, alog_bc, AF.Exp, ["rows"], ["A_bc"])
                stt(dA, dtv, -1.0, A_bc.unsqueeze(1).to_broadcast([128, 10, 64]), ALU.mult, ALU.mult, ["dtv", "A_bc"], ["dA"])
                cp("act", dAh, dA, ["dA"], ["dAh"])
                tt("dve", dAl, dA, dAh, ALU.subtract, ["dA", "dAh"], ["dAl"])
                for d in range(2):
                    outv = PM[d][:, 0:320].rearrange("p (c h) -> p c h", c=10)
                    mm(outv, tri[d], dAh[:, :, d * 32:(d + 1) * 32], True, False, ["dAh", "cb"], [PMK[d]])
                    mm(outv, tri[d], dAl[:, :, d * 32:(d + 1) * 32], False, True, ["dAl", "cb"], [PMK[d]])
                    cp("act", cs[:, d], outv, [PMK[d]], ["cs"])
                    mm(outv, ones_b, dAh[:, :, d * 32:(d + 1) * 32], True, False, ["dAh", "cb"], [PMK[d]])
                    mm(outv, ones_b, dAl[:, :, d * 32:(d + 1) * 32], False, True, ["dAl", "cb"], [PMK[d]])
                    cp("dve", tot[:, d], outv, [PMK[d]], ["tot"])
                kfb = kf_bc.unsqueeze(3).to_broadcast([128, 2, 10, 32])
                act(e_offk, cs, AF.Exp, ["cs"], ["e_offk"])
                tt("dve", e_offk, e_offk, kfb, ALU.mult, ["e_offk", "rows"], ["e_offk"])
                tt("dve", t1, tot, cs, ALU.subtract, ["tot", "cs"], ["t1"])
                act(t1, t1, AF.Exp, ["t1"], ["t1"])
                tt("dve", wst, t1, dt_dch, ALU.mult, ["t1", "dtv"], ["wst"])
                act(kcd, tot, AF.Exp, ["tot"], ["kcd"])
                tt("dve", kcd, kcd, kfb, ALU.mult, ["kcd", "rows"], ["kcd"])
                act(t1, dt_dch, AF.Ln, ["dtv", "wst"], ["t1"])
                tt("dve", negu, t1, cs, ALU.subtract, ["t1", "cs"], ["negu"])
                for src, skey, dst, dkey in ((cs, "cs", cshl, "cshl"), (negu, "negu", nuhl, "nuhl")):
                    dv = dst.rearrange("p c (hl d h) -> p hl d c h", hl=2, d=2)
                    cp("act", dv[:, 0], src, [skey], [dkey])
                    tt("dve", dv[:, 1], src, dv[:, 0], ALU.subtract, [skey, dkey], [dkey])
                for src, skey, dstT, dkey in ((cshl, "cshl", cshlT, "cshlT"), (nuhl, "nuhl", nuhlT, "nuhlT")):
                    for c0 in range(0, NCH, 4):
                        n4 = min(4, NCH - c0)
                        for q in range(n4):
                            tr(PM[0][:, q * 128:(q + 1) * 128], src[:, c0 + q, :], [skey], [PMK[0]], q == 0)
                        cp("dve", dstT[:, c0 * 128:(c0 + n4) * 128], PM[0][:, 0:n4 * 128], [PMK[0]], [dkey])

                inter_z = list(inter_z)
                for pc in range(8):
                    for _ in range(2):
                        if inter_z:
                            inter_z.pop(0)()
                    wv, wk = ws.get(wj[:, pc * 256:(pc + 1) * 256], 8, 256)
                    for oo in range(2):
                        zt = pc * 2 + oo
                        a = zt % 2
                        lin_tile(a, wv, wk, oo * 128, lambda k: hT[:, k, :], lambda k: ("hT", k),
                                 [(k, k) for k in range(8)], True, True)
                        act(ygT[:, zt, :], PA[a][:, 0:T], AF.Silu, PAK[a], [("ygT", zt)])

                P.barrier()
                cbuf = [rstd, cv1]
                conv_n = [0]

                def conv_tile(a, ct, out, okey):
                    q = conv_n[0] % 2
                    conv_n[0] += 1
                    cb_ = cbuf[q]
                    ck_ = ("cacc", q)
                    crk = ("corr", q)
                    ca3 = cb_.rearrange("p (s w) -> p s w", w=64)
                    pa = PA[a][:, 0:T]
                    pa3 = pa.rearrange("p (s w) -> p s w", w=64)
                    pk = PAK[a]
                    act(cb_, pa, AF.Identity, pk + ["sm"], [ck_], bias=convb[:, ct:ct + 1], scale=convw[:, ct, 1:2])
                    stt(cb_[:, 1:T], PA[a][:, 0:T - 1], convw[:, ct, 0:1], cb_[:, 1:T], ALU.mult, ALU.add, pk + [ck_, "sm"], [ck_])
                    stt(cb_[:, 0:T - 1], PA[a][:, 1:T], convw[:, ct, 2:3], cb_[:, 0:T - 1], ALU.mult, ALU.add, pk + [ck_, "sm"], [ck_])
                    tt("dve", corr[:, q, 0, 0:19], pa3[:, 0:19, 63], ncut_bc, ALU.mult, pk + ["rows"], [crk])
                    tt("dve", corr[:, q, 1, 0:19], pa3[:, 1:20, 0], ncut_bc, ALU.mult, pk + ["rows"], [crk])
                    stt(ca3[:, 1:20, 0], corr[:, q, 0, 0:19], convw[:, ct, 0:1], ca3[:, 1:20, 0], ALU.mult, ALU.add, [crk, ck_, "sm"], [ck_])
                    stt(ca3[:, 0:19, 63], corr[:, q, 1, 0:19], convw[:, ct, 2:3], ca3[:, 0:19, 63], ALU.mult, ALU.add, [crk, ck_, "sm"], [ck_])
                    act(out, cb_, AF.Silu, [ck_], [okey])

                tr_n = [0]

                def transposes_to_tm(srcT, skey, dst3, dkey, f0):
                    for c0 in range(0, NCH, 4):
                        n4 = min(4, NCH - c0)
                        pb = tr_n[0] % 2
                        tr_n[0] += 1
                        for q in range(n4):
                            tr(PM[pb][:, q * 128:(q + 1) * 128], srcT[:, (c0 + q) * 128:(c0 + q + 1) * 128], [skey], [PMK[pb]], q == 0)
                        cp("act", dst3[:, c0:c0 + n4, f0:f0 + 128], PM[pb][:, 0:n4 * 128].rearrange("p (c f) -> p c f", c=n4),
                           [PMK[pb]], [dkey])

                for g in range(4):
                    for pc in range(2):
                        wv, wk = ws.get(wj[:, 2048 + g * 512 + pc * 256:2048 + g * 512 + (pc + 1) * 256], 8, 256)
                        for oo in range(2):
                            t_ = pc * 2 + oo
                            a = t_ % 2
                            lin_tile(a, wv, wk, oo * 128, lambda k: hT[:, k, :], lambda k: ("hT", k),
                                     [(k, k) for k in range(8)], True, True)
                            conv_tile(a, g * 4 + t_, xTt[a], ("xTt", a))
                            transposes_to_tm(xTt[a], ("xTt", a), x_tm, "x_tm", t_ * 128)
                    wv, wk = ws.get(wj[:, 4096 + g * 128:4096 + (g + 1) * 128], 8, 128)
                    lin_tile(0, wv, wk, 0, lambda k: hT[:, k, :], lambda k: ("hT", k), [(k, k) for k in range(8)], True, True)
                    conv_tile(0, 16 + g, BgT, "BgT")
                    transposes_to_tm(BgT, "BgT", B_tm, "B_tm", 0)
                    wv, wk = ws.get(wj[:, 4608 + g * 128:4608 + (g + 1) * 128], 8, 128)
                    lin_tile(1, wv, wk, 0, lambda k: hT[:, k, :], lambda k: ("hT", k), [(k, k) for k in range(8)], True, True)
                    conv_tile(1, 20 + g, CgT, "CgT")
                    for c0 in range(0, NCH, 4):
                        n4 = min(4, NCH - c0)
                        for q in range(n4):
                            c = c0 + q
                            mm(PM[0][:, q * 128:(q + 1) * 128], BgT[:, c * 128:(c + 1) * 128], CgT[:, c * 128:(c + 1) * 128],
                               start=(q == 0), stop=(q == n4 - 1), reads=["BgT", "CgT"], writes=[PMK[0]])
                        cp("act", CBT[:, c0:c0 + n4, :], PM[0][:, 0:n4 * 128].rearrange("p (c f) -> p c f", c=n4), [PMK[0]], ["CBT"])

                    hs = slice(g * 8, (g + 1) * 8)

                    HBd = [PA[1][:, 1024:1536], PA[1][:, 512:1024]]
                    HBKd = [PAK[1][2], PAK[1][1]]
                    Yd = [PA[0][:, 0:512], PM[0][:, 0:512]]
                    YKd = [PAK[0][0], PMK[0]]
                    hdir_bf = [hf_bf, LT2]
                    seg_n = [0]

                    def state_zero_bits(d):
                        mm(HBd[d], zeros_b, cb[:, 0:512], True, True, ["cb"], [HBKd[d]])

                    def state_update(d, c):
                        xw = xwt[d]
                        tt("dve", xw.rearrange("p (h q) -> p h q", h=8), x_tm[:, c, :].rearrange("p (h q) -> p h q", h=8),
                           wst[:, d, c, hs].unsqueeze(2).to_broadcast([128, 8, 64]), ALU.mult, ["x_tm", "wst"], [("xw", d)])
                        tt("dve", HBd[d].rearrange("p (h q) -> p h q", h=8), HBd[d].rearrange("p (h q) -> p h q", h=8),
                           kcd[:, d, c, hs].unsqueeze(2).to_broadcast([128, 8, 64]), ALU.mult, [HBKd[d], "kcd"], [HBKd[d]])
                        mm(HBd[d], B_tm[:, c, :], xw, False, False, ["B_tm", ("xw", d)], [HBKd[d]])

                    def state_out(d, u):
                        cp("act", H[d], HBd[d], [HBKd[d]], [("H", d)])
                        P.dma("sp", st_d[u, j, d][:, g * 512:(g + 1) * 512], H[d], reads=[("H", d)], chan=("sto", d))

                    def diag_pass(c):
                        ck = slice(c * 128, (c + 1) * 128)
                        par = c % 2
                        Yb, YK = Yd[par], YKd[par]
                        first = True
                        for d in range(2):
                            for hq in range(2):
                                nseg = (seg_n[0]) % 2
                                seg_n[0] += 1
                                sb = 1 + nseg
                                seg = PA[0][:, sb * 512:(sb + 1) * 512]
                                sk = PAK[0][sb]
                                lt_ = LT[nseg]
                                mt_ = MT[nseg]
                                jj0 = d * 32 + g * 8 + hq * 4
                                for hh in range(4):
                                    mm(seg[:, hh * 128:(hh + 1) * 128], ecol[:, jj0 + hh:jj0 + hh + 1].to_broadcast([128, 128]),
                                       cshlT[:, ck], start=(hh == 0), stop=False, reads=["cb", "cshlT"], writes=[sk])
                                mm(seg.rearrange("p (h l) -> p h l", h=4), nuhlT[:, ck],
                                   ecol[:, jj0:jj0 + 4].unsqueeze(2).to_broadcast([128, 4, 128]), False, False, ["cb", "nuhlT"], [sk])
                                mm(seg, ident, negm[d], False, True, ["cb"], [sk])
                                act(lt_, seg, AF.Exp, [sk], [("LT", nseg)])
                                tt("dve", mt_.rearrange("p (h l) -> p h l", h=4), lt_.rearrange("p (h l) -> p h l", h=4),
                                   CBT[:, c, :].unsqueeze(1).to_broadcast([128, 4, 128]), ALU.mult,
                                   [("LT", nseg), "CBT"], [("MT", nseg)])
                                for hh in range(4):
                                    hl = hq * 4 + hh
                                    mm(Yb[:, hl * 64:(hl + 1) * 64], mt_[:, hh * 128:(hh + 1) * 128], x_tm[:, c, hl * 64:(hl + 1) * 64],
                                       start=first, stop=False, reads=[("MT", nseg), "x_tm"], writes=[YK])
                                    first = False
                        tt("dve", xD.rearrange("p (h q) -> p h q", h=8), x_tm[:, c, :].rearrange("p (h q) -> p h q", h=8),
                           d_bc[:, hs].unsqueeze(2).to_broadcast([128, 8, 64]), ALU.mult, ["x_tm", "rows"], ["xD"])
                        mm(Yb, ident, xD, False, True, ["cb", "xD"], [YK])
                        cp("act", hb_bf[:, c, :], Yb, [YK], [("stash", c)])

                    def sweep_step(d, c, finalize):
                        ck = slice(c * 128, (c + 1) * 128)
                        if d == 1:
                            if c == NCH - 1:
                                state_zero_bits(1)
                            if c == 7:
                                P.dma("sp", H[1], init_d[j, 1][:, g * 512:(g + 1) * 512], writes=[("H", 1)], chan="ldh1")
                                cp("act", HBd[1], H[1], [("H", 1)], [HBKd[1]])
                        else:
                            if c == 0:
                                state_zero_bits(0)
                                P.dma("sp", H[0], init_d[j, 0][:, g * 512:(g + 1) * 512], writes=[("H", 0)], chan="ldh0")
                                cp("act", HBd[0], H[0], [("H", 0)], [HBKd[0]])
                            if c == 8:
                                memset("dve", HBd[0], 0.0, [HBKd[0]])
                        hbf = hdir_bf[d]
                        hbk = ("hdir_bf", d)
                        cp("act", hbf, HBd[d], [HBKd[d]], [hbk])
                        ob = PA[1][:, 0:512]
                        ok_ = PAK[1][0]
                        mm(ob, CgT[:, ck], hbf, True, True, ["CgT", hbk], [ok_])
                        tt("dve", yo[d].rearrange("p (h q) -> p h q", h=8), ob.rearrange("p (h q) -> p h q", h=8),
                           e_offk[:, d, c, hs].unsqueeze(2).to_broadcast([128, 8, 64]), ALU.mult, [ok_, "e_offk"], [("yo", d)])
                        if finalize:
                            par = c % 2
                            y_tm = y_tm2[par]
                            ytk = ("y_tm", par)
                            Yf = PA[0][:, (1 + par) * 512:(2 + par) * 512]
                            YfK = PAK[0][1 + par]
                            mm(Yf, ident, hb_bf[:, c, :], True, False, ["cb", ("stash", c)], [YfK])
                            mm(Yf, ident, yo[d], False, True, ["cb", ("yo", d)], [YfK])
                            cp("act", y_tm, Yf, [YfK], [ytk])
                            for t_ in range(4):
                                tr(PM[1][:, t_ * 128:(t_ + 1) * 128], y_tm[:, t_ * 128:(t_ + 1) * 128], [ytk], [PMK[1]], t_ == 0)
                            ygv = ygT[:, g * 4:(g + 1) * 4, ck]
                            gkeys = [("ygT", g * 4 + t_) for t_ in range(4)]
                            tt("dve", ygv, PM[1][:, 0:512].rearrange("p (t l) -> p t l", t=4), ygv, ALU.mult, [PMK[1]] + gkeys, gkeys)
                        else:
                            tt("dve", hb_bf[:, c, :], hb_bf[:, c, :], yo[d], ALU.add, [("stash", c), ("yo", d)], [("stash", c)])
                        state_update(d, c)
                        if d == 1 and c % 2 == 0:
                            state_out(1, c // 2)
                        if d == 0 and c % 2 == 1:
                            state_out(0, (c - 1) // 2)

                    for stp in range(NCH // 2):
                        diag_pass(stp)
                        diag_pass(NCH - 1 - stp)
                        sweep_step(1, NCH - 1 - stp, finalize=False)
                        sweep_step(0, stp, finalize=False)
                    for stp in range(NCH // 2, NCH):
                        sweep_step(1, NCH - 1 - stp, finalize=True)
                        sweep_step(0, stp, finalize=True)

                P.fill = None
                P.barrier()
                sq2 = [xraw.bitcast(BF16)[:, 0:T], cacc.bitcast(BF16)[:, 0:T]]
                rms_stats(lambda k: ygT[:, k, :], 16, sq2, 1.0 / 2048.0, lambda k: ("ygT", k))
                for k in range(16):
                    tsc("dve", ygT[:, k, :], ygT[:, k, :], normw[:, k:k + 1], ALU.mult, [("ygT", k), "sm"], [("ygT", k)])
                tmpo = [xraw, cacc]
                for ot in range(8):
                    a = ot % 2
                    wv, wk = ws.get(w_out[j][:, ot * 128:(ot + 1) * 128], 16, 128)
                    lin_tile(a, wv, wk, 0, lambda k: ygT[:, k, :], lambda k: ("ygT", k), [(k, k) for k in range(16)], True, True)
                    tt("dve", tmpo[a], PA[a][:, 0:T], rstd, ALU.mult, PAK[a] + ["rstd"], [("tmpo", a)])
                    resid_add(ot, tmpo[a], [("tmpo", a)], mb, mk, 16)

            a0, b0 = ada_steps(0)
            for s in a0:
                s()
            for i in range(n_layers):
                mb, mkA, mkB = mods[i % 2], ("modsA", i % 2), ("modsB", i % 2)
                if i % 2 == 0:
                    ssd(i // 2, i, mb, mkA, b0 if i == 0 else ())
                else:
                    fnet(i // 2, i, mb, mkA)
                inter = []
                if i + 1 < 4:
                    an, bn = ada_steps(i + 1)
                    inter = an + bn
                mlp(i, mb, mkB, inter)
            P.barrier()
            sqf = [view_at(PR0 + q * T * 2, T * 2, BF16) for q in range(2)]
            outf = [view_at(PR0 + 2 * T * 2 + q * T * 4, T * 4, F32) for q in range(2)]
            rms_stats(lambda k: x[:, k, :], 8, sqf, 1.0 / 1024.0, lambda k: ("x", k))
            for k in range(8):
                stt(outf[k % 2], x[:, k, :], sm[:, SM_FIN + k:SM_FIN + k + 1], rstd, ALU.mult, ALU.mult,
                    [("x", k), "sm", "rstd"], [("outf", k % 2)])
                P.dma("sp", yT_d[k * 128:(k + 1) * 128, :], outf[k % 2], reads=[("outf", k % 2)], chan=("out", k % 2))

        Pd = Prog(nc, dry=True)
        wsd = WStream(Pd, slots)
        gen(Pd, wsd)
        P = Prog(nc)
        ws = WStream(P, slots, plan=wsd.rec)
        gen(P, ws)
        assert ws.n == len(wsd.rec)
        P.emit()
        build_program.stats = dict(P.stats, est_us=getattr(P, "est_us", None), n_fill=P.n_fill)
    return nc


def _bf(a):
    return np.asarray(a, dtype=np.float32).astype(ml_dtypes.bfloat16)


def _consts():
    cbm = np.zeros((128, NCB), np.float32)
    cbm[:, CB_ID:CB_ID + 128] = np.eye(128)
    cbm[:, CB_ONES:CB_ONES + 128] = 1.0
    s = np.arange(128)[:, None]
    l = np.arange(128)[None, :]
    cbm[:, CB_TRIF:CB_TRIF + 128] = (s <= l)
    cbm[:, CB_TRIB:CB_TRIB + 128] = (s >= l)
    nf = np.where(s <= l, 0.0, -30000.0)
    nb = np.where(s >= l, 0.0, -30000.0)
    cbm[:, CB_NEGM:CB_NEGM + 512] = np.tile(nf, (1, 4))
    cbm[:, CB_NEGM + 512:CB_NEGM + 1024] = np.tile(nb, (1, 4))
    for jj in range(64):
        cbm[jj, CB_ECOL + jj] = 1.0
        cbm[64 + jj, CB_ECOL + jj] = 1.0
    d = np.arange(256)
    ang = 2.0 * np.pi * np.outer(d, d) / 256.0
    csd = np.concatenate([np.cos(ang), np.sin(ang)], axis=1)

    def dft(L):
        k = np.arange(L)
        a = 2.0 * np.pi * np.outer(k, k) / L
        sc = 1.0 / np.sqrt(L * 256.0)
        return np.stack([np.cos(a) * sc, -np.sin(a) * sc], axis=1)

    d256 = dft(256)
    d1024 = dft(1024)
    blk = np.zeros((1024, 2, 1024))
    for u in range(4):
        blk[u * 256:(u + 1) * 256, :, u * 256:(u + 1) * 256] = d256
    return _bf(cbm), _bf(csd), _bf(d1024), _bf(blk), _bf(d256)


def _col(v, ntiles):
    return np.asarray(v, np.float32).reshape(ntiles, 128).T


_NC_CACHE = {}


def kernel(x_prompt, x_sample, state_ssd, c, c_ctx, ada_w, ada_b, norm_mix_w, norm_mlp_w,
           ssd_w_in, ssd_conv_w, ssd_conv_b, ssd_dt_bias, ssd_a_log, ssd_d, ssd_norm_w,
           ssd_w_out, fno_w_out, fno_b_out, mlp_w1, mlp_w2, final_norm_w):
    f32 = lambda a: np.ascontiguousarray(np.asarray(a, dtype=np.float32))
    x_prompt, x_sample, state_ssd = f32(x_prompt), f32(x_sample), f32(state_ssd)
    c, c_ctx = f32(c), f32(c_ctx)
    cbm, csd, dft_dense, dft_blk, d256 = _consts()

    smalls = np.zeros((128, NSM), np.float32)
    for i in range(4):
        b = i * SM_LAYER
        smalls[:, b:b + 48] = _col(f32(ada_b)[i], 48)
        smalls[:, b + 48:b + 56] = _col(f32(norm_mix_w)[i], 8)
        smalls[:, b + 56:b + 64] = _col(f32(norm_mlp_w)[i], 8)
    for j in range(2):
        b = SM_SSD0 + j * 128
        cw = f32(ssd_conv_w)[j]
        cwt = np.stack([_col(cw[k], 24) for k in range(3)], axis=2)
        smalls[:, b:b + 72] = cwt.reshape(128, 72)
        smalls[:, b + 72:b + 96] = _col(f32(ssd_conv_b)[j], 24)
        smalls[:, b + 96:b + 112] = _col(f32(ssd_norm_w)[j], 16)
        smalls[:, SM_FNO0 + j * 8:SM_FNO0 + j * 8 + 8] = _col(f32(fno_b_out)[j], 8)
    smalls[:, SM_FIN:SM_FIN + 8] = _col(f32(final_norm_w), 8)

    weights = dict(ada_w=f32(ada_w), ssd_w_in=f32(ssd_w_in), ssd_w_out=f32(ssd_w_out), fno_w_out=f32(fno_w_out),
                   mlp_w1=f32(mlp_w1), mlp_w2=f32(mlp_w2))
    in_maps = []
    for core in range(8):
        rows = np.zeros((1, NR), np.float32)
        for j in range(2):
            rows[0, j * RW_SSD:j * RW_SSD + 64] = f32(ssd_dt_bias)[j].reshape(64)
            rows[0, j * RW_SSD + 64:j * RW_SSD + 128] = f32(ssd_a_log)[j].reshape(64)
            rows[0, j * RW_SSD + 128:j * RW_SSD + 160] = f32(ssd_d)[j]
        kf = np.zeros((2, 10), np.float32)
        ncut = np.zeros(19, np.float32)
        if core < 2:
            toks = np.concatenate([x_sample[core], x_prompt[30 + core]], axis=0)
            condA, condB = c[core], c_ctx
            init = np.stack([np.stack([state_ssd[core, j, d].transpose(2, 0, 1).reshape(128, 2048) for d in range(2)])
                             for j in range(2)])
            kf[0] = [1, 1, 1, 1, 1, 1, 1, 1, 1, 1]
            kf[1] = [1, 1, 1, 1, 1, 1, 1, 1, 1, 1]
            ncut[0:16] = -1.0
            dftA = dft_dense
        else:
            p0 = 5 * (core - 2)
            toks = np.concatenate([x_prompt[p0 + u] for u in range(5)], axis=0)
            condA, condB = c_ctx, c_ctx
            init = np.zeros((2, 2, 128, 2048), np.float32)
            kf[0] = [1, 1, 0, 1, 0, 1, 0, 1, 1, 1]
            kf[1] = [1, 0, 1, 0, 1, 0, 1, 1, 1, 1]
            for jb in (4, 8, 12, 16):
                ncut[jb - 1] = -1.0
            dftA = dft_blk
        rows[0, RW_KF:RW_KF + 20] = kf.reshape(20)
        rows[0, RW_CUT:RW_CUT + 19] = ncut
        cond = np.stack([_col(condA, 8), _col(condB, 8)], axis=2).reshape(128, 16)
        m = dict(xT=np.ascontiguousarray(toks.T), condT=np.ascontiguousarray(cond), smalls=smalls, rows=rows,
                 init_st=np.ascontiguousarray(init), cb=cbm, dftA=dftA, dftB=d256, csd=csd)
        m.update(weights)
        in_maps.append(m)

    if "nc" not in _NC_CACHE:
        _NC_CACHE["nc"] = build_program()
    nc = _NC_CACHE["nc"]
    res = run_bass_kernel_spmd(nc, in_maps, core_ids=list(range(8)))

    y_prompt = np.zeros((32, 256, 1024), np.float32)
    y_sample = np.zeros((2, 1024, 1024), np.float32)
    new_state = np.zeros((32, 2, 2, 32, 64, 128), np.float32)
    for core in range(8):
        r = res.results[core]
        y = np.asarray(r["yT"], np.float32).T
        so = np.asarray(r["st_out"], np.float32)

        def put_state(p, u):
            for j in range(2):
                for d in range(2):
                    new_state[p, j, d] = so[u, j, d].reshape(128, 32, 64).transpose(1, 2, 0)

        if core < 2:
            y_sample[core] = y[0:1024]
            y_prompt[30 + core] = y[1024:1280]
            put_state(30 + core, 4)
        else:
            p0 = 5 * (core - 2)
            for u in range(5):
                y_prompt[p0 + u] = y[u * 256:(u + 1) * 256]
                put_state(p0 + u, u)
    return (y_prompt, y_sample, new_state)
```

```python
import contextlib
import numpy as np
import ml_dtypes
import concourse.bass as bass
import concourse.mybir as mybir
from concourse.bass_utils import run_bass_kernel_spmd

F32 = mybir.dt.float32
BF16 = mybir.dt.bfloat16
AF = mybir.ActivationFunctionType
ALU = mybir.AluOpType

ENGS = ("pe", "act", "dve", "pool", "sp")
T = 1280
NCH = 10
EPS = 1e-6
NSLOT = 5
PRIO_CRITICAL_PATH = False
SCHED_WINDOW = 100
FILL_THR = 0.25
FILL_DUR = 0.25
FILL_MAX = 6
SLOT_ELEMS = 2048
RANGES = ((0, 512), (512, 1024), (1024, 1280))
URANGES = ((0, 0, 1024), (1, 1024, 1280))


class Prog:
    def __init__(self, nc, dry=False):
        self.nc = nc
        self.dry = dry
        self.ops = []
        self.last_w = {}
        self.readers = {}
        self.chan_last = {}
        self.chans = []
        self.phase = 0
        self.fill = None
        self.n_fill = 0

    def barrier(self):
        self.phase += 1

    def _add(self, eng, fn, reads, writes, chan=None, cost=0.5):
        if self.dry:
            return -1
        idx = len(self.ops)
        deps = set()
        odeps = set()
        for k in reads:
            w = self.last_w.get(k)
            if w is not None:
                deps.add(w)
        for k in writes:
            w = self.last_w.get(k)
            if w is not None:
                deps.add(w)
            for r in self.readers.get(k, ()):
                deps.add(r)
        if chan is not None:
            if chan not in self.chan_last:
                self.chans.append(chan)
            p = self.chan_last.get(chan)
            if p is not None:
                deps.add(p)
            self.chan_last[chan] = idx
        for k in writes:
            self.last_w[k] = idx
            self.readers[k] = []
        for k in reads:
            if k not in writes:
                lst = self.readers.setdefault(k, [])
                if chan is None:
                    for q, r in enumerate(lst):
                        if self.ops[r]["chan"] is None and self.ops[r]["eng"] == eng:
                            odeps.add(r)
                            lst[q] = idx
                            break
                    else:
                        lst.append(idx)
                else:
                    lst.append(idx)
        deps.discard(idx)
        odeps.discard(idx)
        odeps -= deps
        self.ops.append(dict(eng=eng, fn=fn, deps=sorted(deps), odeps=sorted(odeps), chan=chan, cost=cost,
                             phase=self.phase, fill=(self.fill if (eng == "pe" and chan is None) else None), gap=0.0))
        return idx

    def op(self, eng, fn, reads=(), writes=(), cost=0.5):
        return self._add(eng, fn, tuple(reads), tuple(writes), cost=cost)

    def dma(self, q, out, in_, reads=(), writes=(), chan="d0"):
        nbytes = 4.0
        for d in in_.shape:
            nbytes *= d
        return self._add(q, lambda e: e.dma_start(out=out, in_=in_), tuple(reads), tuple(writes), chan=chan,
                         cost=1.5 + nbytes / 300e3)

    def schedule(self, reorder=True):
        import heapq
        ops = self.ops
        n = len(ops)
        dependents = [[] for _ in range(n)]
        for i, o in enumerate(ops):
            for d in o["deps"]:
                dependents[d].append(i)
            for d in o["odeps"]:
                dependents[d].append(i)
        by_phase = {}
        for i, o in enumerate(ops):
            by_phase.setdefault(o["phase"], []).append(i)
        LAT = 0.1
        bl = [0.0] * n
        for i in range(n - 1, -1, -1):
            m = 0.0
            for j in dependents[i]:
                if bl[j] + LAT > m:
                    m = bl[j] + LAT
            bl[i] = ops[i]["cost"] + m
        if not PRIO_CRITICAL_PATH:
            bl = [0.0] * n
        t_e = {e: 0.0 for e in ENGS}
        fin_t = [None] * n
        order = []
        dma_free = 0.0
        last_sched_eng = {}
        last_sched_chan = {}
        for ph in sorted(by_phase):
            idxs = by_phase[ph]
            B = set(last_sched_eng.values())
            for ch, idx in last_sched_chan.items():
                if not (isinstance(ch, tuple) and ch[0] == "w"):
                    B.add(idx)
            nrem = {}
            ready_t = {}
            pending = {e: [] for e in ENGS}
            avail = {e: [] for e in ENGS}
            for i in idxs:
                o = ops[i]
                if B:
                    o["deps"] = sorted(set(o["deps"]) | B)
                r = 0.0
                cnt = 0
                for d in list(o["deps"]) + list(o["odeps"]):
                    if fin_t[d] is None:
                        cnt += 1
                    else:
                        r = max(r, fin_t[d] + LAT)
                nrem[i] = cnt
                ready_t[i] = r
                if cnt == 0:
                    heapq.heappush(pending[o["eng"]], (r if reorder else 0.0, -bl[i] if reorder else 0.0, i))
            done = 0
            sched_flag = {i: False for i in idxs}
            lo_pos = 0
            while done < len(idxs):
                best = None
                while lo_pos < len(idxs) and sched_flag[idxs[lo_pos]]:
                    lo_pos += 1
                lo_idx = idxs[lo_pos] if lo_pos < len(idxs) else 0
                for use_window in (True, False):
                  for e in ENGS:
                    pe_, av = pending[e], avail[e]
                    while pe_ and pe_[0][0] <= t_e[e]:
                        q_ = heapq.heappop(pe_)
                        heapq.heappush(av, (q_[1], q_[2]))
                    if av and use_window and av[0][1] > lo_idx + SCHED_WINDOW:
                        if pe_ and pe_[0][2] <= lo_idx + SCHED_WINDOW:
                            cand = (pe_[0][0], pe_[0][1], pe_[0][2], e, False)
                        else:
                            continue
                    elif av:
                        cand = (t_e[e], av[0][0], av[0][1], e, True)
                    elif pe_:
                        cand = (pe_[0][0], pe_[0][1], pe_[0][2], e, False)
                    else:
                        continue
                    key = cand[:3] if reorder else (cand[2],)
                    if best is None or key < best[0]:
                        best = (key, cand)
                  if best is not None:
                      break
                start, _pri, i, e, from_av = best[1]
                sched_flag[i] = True
                if from_av:
                    heapq.heappop(avail[e])
                else:
                    heapq.heappop(pending[e])
                o = ops[i]
                if e == "pe":
                    o["gap"] = max(0.0, start - t_e[e])
                start = max(start, t_e[e])
                if o["chan"] is not None:
                    t_e[e] = start + (1.0 if e == "pool" else 0.1)
                    fin = max(start, dma_free) + o["cost"]
                    dma_free = fin - 1.5
                    last_sched_chan[o["chan"]] = i
                else:
                    t_e[e] = start + o["cost"]
                    fin = t_e[e]
                    last_sched_eng[e] = i
                fin_t[i] = fin
                order.append(i)
                done += 1
                for j in dependents[i]:
                    if j in nrem:
                        if fin + LAT > ready_t[j]:
                            ready_t[j] = fin + LAT
                        nrem[j] -= 1
                        if nrem[j] == 0:
                            heapq.heappush(pending[ops[j]["eng"]], (ready_t[j] if reorder else 0.0, -bl[j] if reorder else 0.0, j))
        assert len(order) == n
        self.est_us = max(t_e.values())
        newidx = {old: new for new, old in enumerate(order)}
        new_ops = []
        for old in order:
            o = ops[old]
            o["deps"] = sorted(newidx[d] for d in o["deps"])
            o["odeps"] = sorted(newidx[d] for d in o["odeps"])
            assert all(d < newidx[old] for d in o["deps"]) and all(d < newidx[old] for d in o["odeps"])
            new_ops.append(o)
        self.ops = new_ops

    def emit(self, reorder=True):
        nc = self.nc
        self.schedule(reorder)
        ops = self.ops
        n = len(ops)

        def pe_pe(od, o):
            return od["chan"] is None and od["eng"] == "pe" and o["eng"] == "pe" and o["chan"] is None

        needed = [False] * n
        for i, o in enumerate(ops):
            for d in o["deps"]:
                if pe_pe(ops[d], o):
                    continue
                needed[d] = True
        last_of = {}
        for i, o in enumerate(ops):
            if o["chan"] is None:
                last_of[o["eng"]] = i
        for e, i in last_of.items():
            needed[i] = True
        cnt = {e: 0 for e in ENGS}
        ccnt = {c: 0 for c in self.chans}
        sig = [None] * n
        for i, o in enumerate(ops):
            if o["chan"] is not None:
                ccnt[o["chan"]] += 16
                sig[i] = (("c", o["chan"]), ccnt[o["chan"]])
            elif needed[i]:
                cnt[o["eng"]] += 1
                sig[i] = (("e", o["eng"]), cnt[o["eng"]])
        known = {e: {} for e in ENGS}
        waits = [None] * n
        for i, o in enumerate(ops):
            e = o["eng"]
            w = {}
            for d in o["deps"]:
                if pe_pe(ops[d], o):
                    continue
                sk, v = sig[d]
                if known[e].get(sk, 0) >= v:
                    continue
                if w.get(sk, 0) < v:
                    w[sk] = v
            for sk, v in w.items():
                known[e][sk] = v
            waits[i] = sorted(w.items(), key=lambda t: str(t[0]))
        final = {}
        for i, o in enumerate(ops):
            if sig[i] is not None:
                sk, v = sig[i]
                if final.get(sk, 0) < v:
                    final[sk] = v
        self.stats = dict(n_ops=n, per_eng={e: sum(1 for o in ops if o["eng"] == e) for e in ENGS},
                          n_sig=sum(1 for s in sig if s is not None), n_waits=sum(len(w) for w in waits),
                          maxcnt=dict(cnt))
        with contextlib.ExitStack() as st:
            sems = {}
            for e in ENGS:
                sems[("e", e)] = st.enter_context(nc.semaphore("s_" + e))
            for ci, c in enumerate(self.chans):
                sems[("c", c)] = st.enter_context(nc.semaphore("c_%d" % ci))
            block = st.enter_context(nc.Block())

            def run(engname):
                def body(eng):
                    for i, o in enumerate(ops):
                        if o["eng"] != engname:
                            continue
                        if o["fill"] is not None and o["gap"] > FILL_THR:
                            fo, fl, fr = o["fill"]
                            for _ in range(min(FILL_MAX, int(o["gap"] / FILL_DUR))):
                                eng.matmul(fo, fl, fr, start=True, stop=True, skip_group_check=True)
                                self.n_fill += 1
                        for sk, v in waits[i]:
                            eng.wait_ge(sems[sk], v)
                        ins = o["fn"](eng)
                        if sig[i] is not None:
                            sk, v = sig[i]
                            ins.then_inc(sems[sk], 16 if sk[0] == "c" else 1)
                    if engname == "sp":
                        for sk, v in sorted(final.items(), key=lambda t: str(t[0])):
                            eng.wait_ge(sems[sk], v)
                return body

            block.tensor(run("pe"))
            block.scalar(run("act"))
            block.vector(run("dve"))
            block.gpsimd(run("pool"))
            block.sync(run("sp"))


class WStream:
    def __init__(self, P, slots, plan=None):
        self.P = P
        self.slots = slots
        self.plan = plan
        self.rec = []
        self.n = 0
        self.issued = 0

    def view(self, i, kt, ncols):
        return self.slots[:, i % NSLOT, 0:kt * ncols].rearrange("p (k n) -> p k n", k=kt)

    def get(self, dram_ap, kt, ncols):
        i = self.n
        self.n += 1
        self.rec.append((dram_ap, kt, ncols))
        if self.plan is not None:
            while self.issued < min(len(self.plan), i + NSLOT):
                q = self.issued
                ap, k2, n2 = self.plan[q]
                self.P.dma("pool", self.view(q, k2, n2), ap.rearrange("(k p) n -> p k n", p=128),
                           writes=[("slot", q % NSLOT)], chan=("w", q % NSLOT))
                self.issued += 1
        return self.view(i, kt, ncols), ("slot", i % NSLOT)


SM_LAYER = 64
SM_SSD0 = 4 * SM_LAYER
SM_FNO0 = SM_SSD0 + 2 * 128
SM_FIN = SM_FNO0 + 16
NSM = SM_FIN + 8
RW_SSD = 160
RW_KF = 2 * RW_SSD
RW_CUT = RW_KF + 20
NR = RW_CUT + 20
CB_ID, CB_ONES, CB_TRIF, CB_TRIB, CB_NEGM, CB_ECOL, CB_ZERO = 0, 128, 256, 384, 512, 1536, 1600
NCB = 1728


def build_program(n_layers=4, dbg=False):
    nc = bass.Bass("TRN2", target_bir_lowering=False)
    dr = lambda name, shape, dt: nc.dram_tensor(name, shape, dt, kind="ExternalInput").ap()
    xT_d = dr("xT", [1024, T], F32)
    cond_d = dr("condT", [128, 16], F32)
    sm_d = dr("smalls", [128, NSM], F32)
    rows_d = dr("rows", [1, NR], F32)
    init_d = dr("init_st", [2, 2, 128, 2048], F32)
    cb_d = dr("cb", [128, NCB], BF16)
    dftA_d = dr("dftA", [1024, 2, 1024], BF16)
    dftB_d = dr("dftB", [256, 2, 256], BF16)
    csd_d = dr("csd", [256, 512], BF16)
    ada_w = dr("ada_w", [4, 1024, 6144], F32)
    w_in = dr("ssd_w_in", [2, 1024, 5184], F32)
    w_out = dr("ssd_w_out", [2, 2048, 1024], F32)
    fno_w = dr("fno_w_out", [2, 1024, 1024], F32)
    w1 = dr("mlp_w1", [4, 1024, 4096], F32)
    w2 = dr("mlp_w2", [4, 4096, 1024], F32)
    yT_d = nc.dram_tensor("yT", [1024, T], F32, kind="ExternalOutput").ap()
    st_d = nc.dram_tensor("st_out", [5, 2, 2, 128, 2048], F32, kind="ExternalOutput").ap()

    with contextlib.ExitStack() as st:
        ARENA = 209920
        arena = st.enter_context(nc.sbuf_tensor("arena", [128, ARENA // 2], BF16))
        PA = [st.enter_context(nc.psum_tensor("pa%d" % i, [128, 1536], F32)) for i in range(2)]
        PM = [st.enter_context(nc.psum_tensor("pm%d" % i, [128, 512], F32)) for i in range(2)]

        off = [0]

        def alloc(nbytes, dt, pat=None, **kw):
            o = off[0]
            nb = (nbytes + 63) // 64 * 64
            off[0] = o + nb
            assert off[0] <= ARENA, ("arena overflow", off[0])
            return view_at(o, nbytes, dt, pat, **kw)

        def view_at(o, nbytes, dt, pat=None, **kw):
            v = arena[:, o // 2:(o + nbytes) // 2]
            if dt == F32:
                v = v.bitcast(F32)
            if pat:
                v = v.rearrange(pat, **kw)
            return v

        x = alloc(8 * T * 4, F32, "p (k t) -> p k t", k=8)
        hT = alloc(8 * T * 2, BF16, "p (k t) -> p k t", k=8)
        slots = alloc(NSLOT * SLOT_ELEMS * 2, BF16, "p (s e) -> p s e", s=NSLOT)
        cb = alloc(NCB * 2, BF16)
        sm = alloc(NSM * 4, F32)
        rows = alloc(NR * 4, F32)
        condT = alloc(16 * 4, F32, "p (k u) -> p k u", u=2)
        scond = alloc(16 * 2, BF16, "p (k u) -> p k u", u=2)
        mods = [alloc(64 * 2 * 4, F32, "p (m u) -> p m u", u=2) for _ in range(2)]
        rstd = alloc(T * 4, F32)
        PR0 = off[0]
        PR_BYTES = ARENA - PR0

        ident = cb[:, CB_ID:CB_ID + 128]
        ones_b = cb[:, CB_ONES:CB_ONES + 128]
        tri = [cb[:, CB_TRIF:CB_TRIF + 128], cb[:, CB_TRIB:CB_TRIB + 128]]
        negm = [cb[:, CB_NEGM:CB_NEGM + 512], cb[:, CB_NEGM + 512:CB_NEGM + 1024]]
        ecol = cb[:, CB_ECOL:CB_ECOL + 64]
        zeros_b = cb[:, CB_ZERO:CB_ZERO + 128]

        def pbank(acc, b):
            return ("pa", acc, b)

        PAK = [[pbank(a, b) for b in range(3)] for a in range(2)]
        PMK = [("pm", 0), ("pm", 1)]

        def gen(P, ws):
            def fsz(ap):
                n_ = 1
                for d_ in ap.shape[1:]:
                    n_ *= d_
                return n_

            def mm(out, lhsT, rhs, start, stop, reads, writes):
                P.op("pe", lambda e: e.matmul(out, lhsT, rhs, start=start, stop=stop, skip_group_check=True), reads, writes,
                     cost=0.06 + fsz(out) / 2400.0)

            def tr(out, in_, reads, writes, start):
                P.op("pe", lambda e: e.matmul(out, in_, ident, start=start, stop=True, skip_group_check=True),
                     tuple(reads) + ("cb",), writes, cost=0.12)

            def act(out, in_, func, reads, writes, bias=None, scale=None):
                kw = {}
                if bias is not None:
                    kw["bias"] = bias
                if scale is not None:
                    kw["scale"] = scale
                P.op("act", lambda e: e.activation(out=out, in_=in_, func=func, **kw), reads, writes,
                     cost=0.22 + fsz(out) / 1200.0)

            def vcost(eng, out):
                return (0.12 + fsz(out) / 900.0) * (2.0 if eng == "pool" else 1.0)

            def tt(eng, out, in0, in1, op, reads, writes):
                P.op(eng, lambda e: e.tensor_tensor(out=out, in0=in0, in1=in1, op=op), reads, writes, cost=vcost(eng, out))

            def stt(out, in0, scalar, in1, op0, op1, reads, writes):
                P.op("dve", lambda e: e.scalar_tensor_tensor(out=out, in0=in0, scalar=scalar, in1=in1, op0=op0, op1=op1), reads, writes,
                     cost=vcost("dve", out))

            def cp(eng, out, in_, reads, writes):
                if eng == "act":
                    P.op("act", lambda e: e.activation(out=out, in_=in_, func=AF.Copy), reads, writes,
                         cost=0.22 + fsz(out) / 1200.0)
                else:
                    P.op(eng, lambda e: e.tensor_copy(out=out, in_=in_), reads, writes, cost=vcost(eng, out))

            def tsc(eng, out, in0, s1, op0, reads, writes):
                P.op(eng, lambda e: e.tensor_scalar(out=out, in0=in0, scalar1=s1, scalar2=None, op0=op0), reads, writes,
                     cost=vcost(eng, out))

            def memset(eng, out, val, writes):
                P.op(eng, lambda e: e.memset(out, val), (), writes, cost=vcost(eng, out))

            P.dma("sp", cb, cb_d, writes=["cb"], chan="ld0")
            P.dma("sp", sm, sm_d, writes=["sm"], chan="ld1")
            P.dma("sp", rows, rows_d.partition_broadcast(128), writes=["rows"], chan="ld2")
            P.dma("sp", condT.rearrange("p k u -> p (k u)"), cond_d, writes=["cond"], chan="ld3")
            for k in range(8):
                P.dma("sp", x[:, k, :], xT_d[k * 128:(k + 1) * 128, :], writes=[("x", k)], chan=("ldx", k % 4))
            act(scond, condT, AF.Silu, ["cond"], ["scond"])

            def ada_steps(i):
                mb = mods[i % 2]
                halves = []
                for half in range(2):
                    mk = ("modsA" if half == 0 else "modsB", i % 2)
                    steps = []
                    for pq in range(12):
                        def step(pq=pq, half=half, mk=mk):
                            pc = half * 12 + pq
                            wv, wk = ws.get(ada_w[i][:, pc * 256:(pc + 1) * 256], 8, 256)
                            for o in range(2):
                                ot = pc * 2 + o
                                for k in range(8):
                                    mm(PM[0][:, ot * 2:ot * 2 + 2], wv[:, k, o * 128:(o + 1) * 128], scond[:, k, :],
                                       start=(pq == 0 and o == 0 and k == 0), stop=(pq == 11 and o == 1 and k == 7),
                                       reads=[wk, "scond"], writes=[PMK[0]])
                            if pq == 11:
                                c0 = half * 24
                                adab = sm[:, i * SM_LAYER + c0:i * SM_LAYER + c0 + 24]
                                tt("dve", mb[:, c0:c0 + 24, :], PM[0][:, 2 * c0:2 * c0 + 48].rearrange("p (m u) -> p m u", u=2),
                                   adab.unsqueeze(2).to_broadcast([128, 24, 2]), ALU.add, [PMK[0], "sm"], [mk])
                                nw = sm[:, i * SM_LAYER + 48 + 8 * half:i * SM_LAYER + 56 + 8 * half]
                                stt(mb[:, 48 + 8 * half:56 + 8 * half, :], mb[:, 8 + 24 * half:16 + 24 * half, :], 1.0,
                                    nw.unsqueeze(2).to_broadcast([128, 8, 2]), ALU.add, ALU.mult, [mk, "sm"], [mk])
                        steps.append(step)
                    halves.append(steps)
                return halves

            def rms_stats(src_tiles, nk, sqbuf, inv_n, src_keys, sq_func_in_bf16=False):
                for k in range(nk):
                    sq = sqbuf[k % 2]
                    sqk = ("sq", k % 2)
                    act(sq, src_tiles(k), AF.Square, [src_keys(k)], [sqk])
                    for r, (r0, r1) in enumerate(RANGES):
                        mm(PA[0][:, r0:r1], ones_b, sq[:, r0:r1], start=(k == 0), stop=(k == nk - 1),
                           reads=[sqk, "cb"], writes=[PAK[0][r]])
                act(rstd, PA[0][:, 0:T], AF.Sqrt, PAK[0], ["rstd"], bias=EPS, scale=inv_n)
                P.op("dve", lambda e: e.reciprocal(out=rstd, in_=rstd), ["rstd"], ["rstd"], cost=1.5)

            def norm_mod(mb, mk, a_off, b_off, sqbuf, tmpf):
                rms_stats(lambda k: x[:, k, :], 8, sqbuf, 1.0 / 1024.0, lambda k: ("x", k))
                for k in range(8):
                    tf = tmpf[k % 2]
                    tk = ("tmpf", k % 2)
                    for u, r0, r1 in URANGES:
                        stt(tf[:, r0:r1], x[:, k, r0:r1], mb[:, a_off + k, u:u + 1], rstd[:, r0:r1], ALU.mult, ALU.mult,
                            [("x", k), mk, "rstd"], [tk])
                    for u, r0, r1 in URANGES:
                        act(hT[:, k, r0:r1], tf[:, r0:r1], AF.Identity, [tk, mk], [("hT", k)], bias=mb[:, b_off + k, u:u + 1])

            def resid_add(ot, src, src_keys, mb, mk, g_off):
                for u, r0, r1 in URANGES:
                    stt(x[:, ot, r0:r1], src[:, r0:r1], mb[:, g_off + ot, u:u + 1], x[:, ot, r0:r1], ALU.mult, ALU.add,
                        list(src_keys) + [mk, ("x", ot)], [("x", ot)])

            def lin_tile(acc, wv, wk, col0, in_tile, in_key, k_list, first, last):
                for ki, (kslot, kin) in enumerate(k_list):
                    for r, (r0, r1) in enumerate(RANGES):
                        mm(PA[acc][:, r0:r1], wv[:, kslot, col0:col0 + 128], in_tile(kin)[:, r0:r1],
                           start=(first and ki == 0), stop=(last and ki == len(k_list) - 1),
                           reads=[wk, in_key(kin)], writes=[PAK[acc][r]])

            def mlp(i, mb, mk, inter_steps):
                o = PR0
                h1T = view_at(o, 32 * T * 2, BF16, "p (k t) -> p k t", k=32)
                o += 32 * T * 2
                rt = [view_at(o + q * T * 4, T * 4, F32) for q in range(2)]
                sqbuf = [view_at(PR0 + q * T * 2, T * 2, BF16) for q in range(2)]
                P.barrier()
                norm_mod(mb, mk, 56, 24, sqbuf, rt)
                inter = list(inter_steps)
                for pc in range(16):
                    wv, wk = ws.get(w1[i][:, pc * 256:(pc + 1) * 256], 8, 256)
                    for oo in range(2):
                        ft = pc * 2 + oo
                        a = ft % 2
                        lin_tile(a, wv, wk, oo * 128, lambda k: hT[:, k, :], lambda k: ("hT", k),
                                 [(k, k) for k in range(8)], True, True)
                        act(rt[a], PA[a][:, 0:T], AF.Relu, PAK[a], [("rt", a)])
                        tt("dve", h1T[:, ft, :], rt[a], rt[a], ALU.mult, [("rt", a)], [("h1T", ft)])
                    if inter:
                        inter.pop(0)()
                for ot in range(8):
                    a = ot % 2
                    for half in range(2):
                        wv, wk = ws.get(w2[i][half * 2048:(half + 1) * 2048, ot * 128:(ot + 1) * 128], 16, 128)
                        lin_tile(a, wv, wk, 0, lambda k: h1T[:, k, :], lambda k: ("h1T", k),
                                 [(kk, half * 16 + kk) for kk in range(16)], half == 0, half == 1)
                        if inter:
                            inter.pop(0)()
                    resid_add(ot, PA[a], PAK[a], mb, mk, 40)
                while inter:
                    inter.pop(0)()

            def fnet(j, i, mb, mk):
                o = PR0
                Y = view_at(o, 10 * 4 * 512 * 2, BF16, "p (t g c) -> p t g c", t=10, g=4)
                sqbuf = [view_at(o + q * T * 2, T * 2, BF16) for q in range(2)]
                o += 10 * 4 * 512 * 2
                fT = view_at(o, 8 * T * 2, BF16, "p (k t) -> p k t", k=8)
                o += 8 * T * 2
                dA_s = view_at(o, 8 * 2 * 1024 * 2, BF16, "p (l c k) -> p l c k", l=8, c=2)
                o += 8 * 2 * 1024 * 2
                dB_s = view_at(o, 2 * 2 * 256 * 2, BF16, "p (l c k) -> p l c k", l=2, c=2)
                o += 2 * 2 * 256 * 2
                csd_s = view_at(o, 2 * 512 * 2, BF16, "p (l c) -> p l c", l=2)
                o += 2 * 512 * 2
                assert o <= ARENA, o
                tmpf = [view_at(PR0 + 2 * T * 2 + q * T * 4, T * 4, F32) for q in range(2)]
                P.barrier()
                for l in range(8):
                    P.dma("sp", dA_s[:, l], dftA_d[l * 128:(l + 1) * 128], writes=[("dftA", l)], chan=("ldf", l % 4))
                for l in range(2):
                    P.dma("sp", dB_s[:, l], dftB_d[l * 128:(l + 1) * 128], writes=["dftB"], chan=("ldf", l))
                    P.dma("sp", csd_s[:, l], csd_d[l * 128:(l + 1) * 128], writes=["csd"], chan=("ldf", 2 + l))
                norm_mod(mb, mk, 48, 0, sqbuf, tmpf)
                P.barrier()
                n = 0
                for tt_ in range(10):
                    for gc in range(4):
                        b = n % 2
                        for kt in range(2):
                            mm(PM[b][:, 0:512], hT[:, 2 * gc + kt, tt_ * 128:(tt_ + 1) * 128], csd_s[:, kt, :],
                               start=(kt == 0), stop=(kt == 1), reads=[("hT", 2 * gc + kt), "csd"], writes=[PMK[b]])
                        cp("act" if n % 2 == 0 else "dve", Y[:, tt_, gc, :], PM[b][:, 0:512], [PMK[b]], [("Y", tt_)])
                        n += 1
                for e_ in range(8):
                    gc, half = e_ // 2, e_ % 2
                    a = e_ % 2
                    for r in range(2):
                        cnt = 0
                        for lt in range(8):
                            for cs_ in range(2):
                                mm(PA[a][:, r * 512:(r + 1) * 512], Y[:, lt, gc, cs_ * 256 + half * 128:cs_ * 256 + half * 128 + 128],
                                   dA_s[:, lt, cs_, r * 512:(r + 1) * 512], start=(cnt == 0), stop=(cnt == 15),
                                   reads=[("Y", lt), ("dftA", lt)], writes=[PAK[a][r]])
                                cnt += 1
                    cnt = 0
                    for lt in range(8, 10):
                        for cs_ in range(2):
                            mm(PA[a][:, 1024:1280], Y[:, lt, gc, cs_ * 256 + half * 128:cs_ * 256 + half * 128 + 128],
                               dB_s[:, lt - 8, cs_, :], start=(cnt == 0), stop=(cnt == 3),
                               reads=[("Y", lt), "dftB"], writes=[PAK[a][2]])
                            cnt += 1
                    cp("act" if e_ % 2 == 0 else "dve", fT[:, e_, :], PA[a][:, 0:T], PAK[a], [("fT", e_)])
                P.barrier()
                for pc in range(4):
                    wv, wk = ws.get(fno_w[j][:, pc * 256:(pc + 1) * 256], 8, 256)
                    for oo in range(2):
                        ot = pc * 2 + oo
                        a = ot % 2
                        lin_tile(a, wv, wk, oo * 128, lambda k: fT[:, k, :], lambda k: ("fT", k),
                                 [(k, k) for k in range(8)], True, True)
                        act(tmpf[a], PA[a][:, 0:T], AF.Identity, PAK[a] + ["sm"], [("tmpf", a)],
                            bias=sm[:, SM_FNO0 + j * 8 + ot:SM_FNO0 + j * 8 + ot + 1])
                        resid_add(ot, tmpf[a], [("tmpf", a)], mb, mk, 16)

            def ssd(j, i, mb, mk, inter_z=()):
                smb = SM_SSD0 + j * 128
                convw = sm[:, smb:smb + 72].rearrange("p (t k) -> p t k", k=3)
                convb = sm[:, smb + 72:smb + 96]
                normw = sm[:, smb + 96:smb + 112]
                rwb = j * RW_SSD
                dtb_bc = rows[:, rwb:rwb + 64]
                alog_bc = rows[:, rwb + 64:rwb + 128]
                d_bc = rows[:, rwb + 128:rwb + 160]
                kf_bc = rows[:, RW_KF:RW_KF + 20].rearrange("p (d c) -> p d c", d=2)
                ncut_bc = rows[:, RW_CUT:RW_CUT + 19]
                wj = w_in[j]

                o = [PR0]

                def al(nbytes, dt, pat=None, **kw):
                    v = view_at(o[0], nbytes, dt, pat, **kw)
                    o[0] += (nbytes + 63) // 64 * 64
                    assert o[0] <= ARENA, ("ssd region overflow", o[0])
                    return v

                ygT = al(16 * T * 2, BF16, "p (k t) -> p k t", k=16)
                BgT = al(T * 2, BF16)
                CgT = al(T * 2, BF16)
                xTt = [al(T * 2, BF16) for _ in range(2)]
                x_tm = al(10 * 512 * 2, BF16, "p (c f) -> p c f", c=10)
                B_tm = al(10 * 128 * 2, BF16, "p (c f) -> p c f", c=10)
                CBT = al(10 * 128 * 2, BF16, "p (c f) -> p c f", c=10)
                hb_bf = al(10 * 512 * 2, BF16, "p (c f) -> p c f", c=10)
                hf_bf = al(512 * 2, BF16)
                LT2 = al(512 * 2, BF16)
                H = [al(512 * 4, F32) for _ in range(2)]
                LT = [al(512 * 2, BF16) for _ in range(2)]
                MT = [al(512 * 2, BF16) for _ in range(2)]
                yo = [al(512 * 2, BF16) for _ in range(2)]
                xwt = [al(512 * 2, BF16) for _ in range(2)]
                xD = al(512 * 2, BF16)
                y_tm2 = [al(512 * 2, BF16) for _ in range(2)]
                cv1 = al(T * 4, F32)
                e_offk = al(640 * 4, F32, "p (d c h) -> p d c h", d=2, c=10)
                wst = al(640 * 4, F32, "p (d c h) -> p d c h", d=2, c=10)
                kcd = al(640 * 4, F32, "p (d c h) -> p d c h", d=2, c=10)
                cshlT = al(T * 2, BF16)
                nuhlT = al(T * 2, BF16)
                corr = al(2 * 2 * 20 * 4, F32, "p (q a b) -> p q a b", q=2, a=2)
                hb_off = PR0 + 16 * T * 2 + 4 * ((T * 2 + 63) // 64 * 64) + 10 * 512 * 2 + 2 * 10 * 128 * 2
                xraw = view_at(hb_off, T * 4, F32)
                cacc = view_at(hb_off + T * 4, T * 4, F32)
                so = [PR0 + 16 * T * 2 + 4 * ((T * 2 + 63) // 64 * 64)]

                def sal(nbytes, dt, pat=None, **kw):
                    v = view_at(so[0], nbytes, dt, pat, **kw)
                    so[0] += (nbytes + 63) // 64 * 64
                    return v

                dtv = sal(640 * 4, F32, "p (c j) -> p c j", c=10)
                dA = sal(640 * 4, F32, "p (c j) -> p c j", c=10)
                dAh = sal(640 * 2, BF16, "p (c j) -> p c j", c=10)
                dAl = sal(640 * 2, BF16, "p (c j) -> p c j", c=10)
                A_bc = sal(64 * 4, F32)
                cs = sal(640 * 4, F32, "p (d c h) -> p d c h", d=2, c=10)
                tot = sal(640 * 4, F32, "p (d c h) -> p d c h", d=2, c=10)
                t1 = sal(640 * 4, F32, "p (d c h) -> p d c h", d=2, c=10)
                negu = sal(640 * 4, F32, "p (d c h) -> p d c h", d=2, c=10)
                cshl = sal(10 * 128 * 2, BF16, "p (c f) -> p c f", c=10)
                nuhl = sal(10 * 128 * 2, BF16, "p (c f) -> p c f", c=10)
                assert so[0] <= PR0 + 16 * T * 2 + 4 * ((T * 2 + 63) // 64 * 64) + 10 * 512 * 2 * 2 + 2 * 10 * 128 * 2 + 1024, so[0]
                dt_dch = dtv.rearrange("p c (d h) -> p d c h", d=2)

                sqbuf = [xraw.bitcast(BF16)[:, 0:T], cacc.bitcast(BF16)[:, 0:T]]
                P.barrier()
                norm_mod(mb, mk, 48, 0, sqbuf, [xraw, cacc])
                P.barrier()

                wv, wk = ws.get(wj[:, 5120:5184], 8, 64)
                for c in range(NCH):
                    bnk = 0 if c < 8 else 1
                    cc = c if c < 8 else c - 8
                    for k in range(8):
                        mm(PM[bnk][:, cc * 64:(cc + 1) * 64], hT[:, k, c * 128:(c + 1) * 128], wv[:, k, 0:64],
                           start=(cc == 0 and k == 0), stop=(k == 7), reads=[wk, ("hT", k)], writes=[PMK[bnk]])
                tt("dve", dtv[:, 0:8, :], PM[0][:, 0:512].rearrange("p (c j) -> p c j", c=8),
                   dtb_bc.unsqueeze(1).to_broadcast([128, 8, 64]), ALU.add, [PMK[0], "rows"], ["dtv"])
                tt("dve", dtv[:, 8:10, :], PM[1][:, 0:128].rearrange("p (c j) -> p c j", c=2),
                   dtb_bc.unsqueeze(1).to_broadcast([128, 2, 64]), ALU.add, [PMK[1], "rows"], ["dtv"])
                act(dtv, dtv, AF.Exp, ["dtv"], ["dtv"])
                act(dtv, dtv, AF.Ln, ["dtv"], ["dtv"], bias=1.0)
                act(A_bc, alog_bc, AF.Exp, ["rows"], ["A_bc"])
                stt(dA, dtv, -1.0, A_bc.unsqueeze(1).to_broadcast([128, 10, 64]), ALU.mult, ALU.mult, ["dtv", "A_bc"], ["dA"])
                cp("act", dAh, dA, ["dA"], ["dAh"])
                tt("dve", dAl, dA, dAh, ALU.subtract, ["dA", "dAh"], ["dAl"])
                for d in range(2):
                    outv = PM[d][:, 0:320].rearrange("p (c h) -> p c h", c=10)
                    mm(outv, tri[d], dAh[:, :, d * 32:(d + 1) * 32], True, False, ["dAh", "cb"], [PMK[d]])
                    mm(outv, tri[d], dAl[:, :, d * 32:(d + 1) * 32], False, True, ["dAl", "cb"], [PMK[d]])
                    cp("act", cs[:, d], outv, [PMK[d]], ["cs"])
                    mm(outv, ones_b, dAh[:, :, d * 32:(d + 1) * 32], True, False, ["dAh", "cb"], [PMK[d]])
                    mm(outv, ones_b, dAl[:, :, d * 32:(d + 1) * 32], False, True, ["dAl", "cb"], [PMK[d]])
                    cp("dve", tot[:, d], outv, [PMK[d]], ["tot"])
                kfb = kf_bc.unsqueeze(3).to_broadcast([128, 2, 10, 32])
                act(e_offk, cs, AF.Exp, ["cs"], ["e_offk"])
                tt("dve", e_offk, e_offk, kfb, ALU.mult, ["e_offk", "rows"], ["e_offk"])
                tt("dve", t1, tot, cs, ALU.subtract, ["tot", "cs"], ["t1"])
                act(t1, t1, AF.Exp, ["t1"], ["t1"])
                tt("dve", wst, t1, dt_dch, ALU.mult, ["t1", "dtv"], ["wst"])
                act(kcd, tot, AF.Exp, ["tot"], ["kcd"])
                tt("dve", kcd, kcd, kfb, ALU.mult, ["kcd", "rows"], ["kcd"])
                act(t1, dt_dch, AF.Ln, ["dtv", "wst"], ["t1"])
                tt("dve", negu, t1, cs, ALU.subtract, ["t1", "cs"], ["negu"])
                for src, skey, dst, dkey in ((cs, "cs", cshl, "cshl"), (negu, "negu", nuhl, "nuhl")):
                    dv = dst.rearrange("p c (hl d h) -> p hl d c h", hl=2, d=2)
                    cp("act", dv[:, 0], src, [skey], [dkey])
                    tt("dve", dv[:, 1], src, dv[:, 0], ALU.subtract, [skey, dkey], [dkey])
                for src, skey, dstT, dkey in ((cshl, "cshl", cshlT, "cshlT"), (nuhl, "nuhl", nuhlT, "nuhlT")):
                    for c0 in range(0, NCH, 4):
                        n4 = min(4, NCH - c0)
                        for q in range(n4):
                            tr(PM[0][:, q * 128:(q + 1) * 128], src[:, c0 + q, :], [skey], [PMK[0]], q == 0)
                        cp("dve", dstT[:, c0 * 128:(c0 + n4) * 128], PM[0][:, 0:n4 * 128], [PMK[0]], [dkey])

                inter_z = list(inter_z)
                for pc in range(8):
                    for _ in range(2):
                        if inter_z:
                            inter_z.pop(0)()
                    wv, wk = ws.get(wj[:, pc * 256:(pc + 1) * 256], 8, 256)
                    for oo in range(2):
                        zt = pc * 2 + oo
                        a = zt % 2
                        lin_tile(a, wv, wk, oo * 128, lambda k: hT[:, k, :], lambda k: ("hT", k),
                                 [(k, k) for k in range(8)], True, True)
                        act(ygT[:, zt, :], PA[a][:, 0:T], AF.Silu, PAK[a], [("ygT", zt)])

                P.barrier()
                cbuf = [rstd, cv1]
                conv_n = [0]

                def conv_tile(a, ct, out, okey):
                    q = conv_n[0] % 2
                    conv_n[0] += 1
                    cb_ = cbuf[q]
                    ck_ = ("cacc", q)
                    crk = ("corr", q)
                    ca3 = cb_.rearrange("p (s w) -> p s w", w=64)
                    pa = PA[a][:, 0:T]
                    pa3 = pa.rearrange("p (s w) -> p s w", w=64)
                    pk = PAK[a]
                    act(cb_, pa, AF.Identity, pk + ["sm"], [ck_], bias=convb[:, ct:ct + 1], scale=convw[:, ct, 1:2])
                    stt(cb_[:, 1:T], PA[a][:, 0:T - 1], convw[:, ct, 0:1], cb_[:, 1:T], ALU.mult, ALU.add, pk + [ck_, "sm"], [ck_])
                    stt(cb_[:, 0:T - 1], PA[a][:, 1:T], convw[:, ct, 2:3], cb_[:, 0:T - 1], ALU.mult, ALU.add, pk + [ck_, "sm"], [ck_])
                    tt("dve", corr[:, q, 0, 0:19], pa3[:, 0:19, 63], ncut_bc, ALU.mult, pk + ["rows"], [crk])
                    tt("dve", corr[:, q, 1, 0:19], pa3[:, 1:20, 0], ncut_bc, ALU.mult, pk + ["rows"], [crk])
                    stt(ca3[:, 1:20, 0], corr[:, q, 0, 0:19], convw[:, ct, 0:1], ca3[:, 1:20, 0], ALU.mult, ALU.add, [crk, ck_, "sm"], [ck_])
                    stt(ca3[:, 0:19, 63], corr[:, q, 1, 0:19], convw[:, ct, 2:3], ca3[:, 0:19, 63], ALU.mult, ALU.add, [crk, ck_, "sm"], [ck_])
                    act(out, cb_, AF.Silu, [ck_], [okey])

                tr_n = [0]

                def transposes_to_tm(srcT, skey, dst3, dkey, f0):
                    for c0 in range(0, NCH, 4):
                        n4 = min(4, NCH - c0)
                        pb = tr_n[0] % 2
                        tr_n[0] += 1
                        for q in range(n4):
                            tr(PM[pb][:, q * 128:(q + 1) * 128], srcT[:, (c0 + q) * 128:(c0 + q + 1) * 128], [skey], [PMK[pb]], q == 0)
                        cp("act", dst3[:, c0:c0 + n4, f0:f0 + 128], PM[pb][:, 0:n4 * 128].rearrange("p (c f) -> p c f", c=n4),
                           [PMK[pb]], [dkey])

                for g in range(4):
                    for pc in range(2):
                        wv, wk = ws.get(wj[:, 2048 + g * 512 + pc * 256:2048 + g * 512 + (pc + 1) * 256], 8, 256)
                        for oo in range(2):
                            t_ = pc * 2 + oo
                            a = t_ % 2
                            lin_tile(a, wv, wk, oo * 128, lambda k: hT[:, k, :], lambda k: ("hT", k),
                                     [(k, k) for k in range(8)], True, True)
                            conv_tile(a, g * 4 + t_, xTt[a], ("xTt", a))
                            transposes_to_tm(xTt[a], ("xTt", a), x_tm, "x_tm", t_ * 128)
                    wv, wk = ws.get(wj[:, 4096 + g * 128:4096 + (g + 1) * 128], 8, 128)
                    lin_tile(0, wv, wk, 0, lambda k: hT[:, k, :], lambda k: ("hT", k), [(k, k) for k in range(8)], True, True)
                    conv_tile(0, 16 + g, BgT, "BgT")
                    transposes_to_tm(BgT, "BgT", B_tm, "B_tm", 0)
                    wv, wk = ws.get(wj[:, 4608 + g * 128:4608 + (g + 1) * 128], 8, 128)
                    lin_tile(1, wv, wk, 0, lambda k: hT[:, k, :], lambda k: ("hT", k), [(k, k) for k in range(8)], True, True)
                    conv_tile(1, 20 + g, CgT, "CgT")
                    for c0 in range(0, NCH, 4):
                        n4 = min(4, NCH - c0)
                        for q in range(n4):
                            c = c0 + q
                            mm(PM[0][:, q * 128:(q + 1) * 128], BgT[:, c * 128:(c + 1) * 128], CgT[:, c * 128:(c + 1) * 128],
                               start=(q == 0), stop=(q == n4 - 1), reads=["BgT", "CgT"], writes=[PMK[0]])
                        cp("act", CBT[:, c0:c0 + n4, :], PM[0][:, 0:n4 * 128].rearrange("p (c f) -> p c f", c=n4), [PMK[0]], ["CBT"])

                    hs = slice(g * 8, (g + 1) * 8)

                    HBd = [PA[1][:, 1024:1536], PA[1][:, 512:1024]]
                    HBKd = [PAK[1][2], PAK[1][1]]
                    Yd = [PA[0][:, 0:512], PM[0][:, 0:512]]
                    YKd = [PAK[0][0], PMK[0]]
                    hdir_bf = [hf_bf, LT2]
                    seg_n = [0]

                    def state_zero_bits(d):
                        mm(HBd[d], zeros_b, cb[:, 0:512], True, True, ["cb"], [HBKd[d]])

                    def state_update(d, c):
                        xw = xwt[d]
                        tt("dve", xw.rearrange("p (h q) -> p h q", h=8), x_tm[:, c, :].rearrange("p (h q) -> p h q", h=8),
                           wst[:, d, c, hs].unsqueeze(2).to_broadcast([128, 8, 64]), ALU.mult, ["x_tm", "wst"], [("xw", d)])
                        tt("dve", HBd[d].rearrange("p (h q) -> p h q", h=8), HBd[d].rearrange("p (h q) -> p h q", h=8),
                           kcd[:, d, c, hs].unsqueeze(2).to_broadcast([128, 8, 64]), ALU.mult, [HBKd[d], "kcd"], [HBKd[d]])
                        mm(HBd[d], B_tm[:, c, :], xw, False, False, ["B_tm", ("xw", d)], [HBKd[d]])

                    def state_out(d, u):
                        cp("act", H[d], HBd[d], [HBKd[d]], [("H", d)])
                        P.dma("sp", st_d[u, j, d][:, g * 512:(g + 1) * 512], H[d], reads=[("H", d)], chan=("sto", d))

                    def diag_pass(c):
                        ck = slice(c * 128, (c + 1) * 128)
                        par = c % 2
                        Yb, YK = Yd[par], YKd[par]
                        first = True
                        for d in range(2):
                            for hq in range(2):
                                nseg = (seg_n[0]) % 2
                                seg_n[0] += 1
                                sb = 1 + nseg
                                seg = PA[0][:, sb * 512:(sb + 1) * 512]
                                sk = PAK[0][sb]
                                lt_ = LT[nseg]
                                mt_ = MT[nseg]
                                jj0 = d * 32 + g * 8 + hq * 4
                                for hh in range(4):
                                    mm(seg[:, hh * 128:(hh + 1) * 128], ecol[:, jj0 + hh:jj0 + hh + 1].to_broadcast([128, 128]),
                                       cshlT[:, ck], start=(hh == 0), stop=False, reads=["cb", "cshlT"], writes=[sk])
                                mm(seg.rearrange("p (h l) -> p h l", h=4), nuhlT[:, ck],
                                   ecol[:, jj0:jj0 + 4].unsqueeze(2).to_broadcast([128, 4, 128]), False, False, ["cb", "nuhlT"], [sk])
                                mm(seg, ident, negm[d], False, True, ["cb"], [sk])
                                act(lt_, seg, AF.Exp, [sk], [("LT", nseg)])
                                tt("dve", mt_.rearrange("p (h l) -> p h l", h=4), lt_.rearrange("p (h l) -> p h l", h=4),
                                   CBT[:, c, :].unsqueeze(1).to_broadcast([128, 4, 128]), ALU.mult,
                                   [("LT", nseg), "CBT"], [("MT", nseg)])
                                for hh in range(4):
                                    hl = hq * 4 + hh
                                    mm(Yb[:, hl * 64:(hl + 1) * 64], mt_[:, hh * 128:(hh + 1) * 128], x_tm[:, c, hl * 64:(hl + 1) * 64],
                                       start=first, stop=False, reads=[("MT", nseg), "x_tm"], writes=[YK])
                                    first = False
                        tt("dve", xD.rearrange("p (h q) -> p h q", h=8), x_tm[:, c, :].rearrange("p (h q) -> p h q", h=8),
                           d_bc[:, hs].unsqueeze(2).to_broadcast([128, 8, 64]), ALU.mult, ["x_tm", "rows"], ["xD"])
                        mm(Yb, ident, xD, False, True, ["cb", "xD"], [YK])
                        cp("act", hb_bf[:, c, :], Yb, [YK], [("stash", c)])

                    def sweep_step(d, c, finalize):
                        ck = slice(c * 128, (c + 1) * 128)
                        if d == 1:
                            if c == NCH - 1:
                                state_zero_bits(1)
                            if c == 7:
                                P.dma("sp", H[1], init_d[j, 1][:, g * 512:(g + 1) * 512], writes=[("H", 1)], chan="ldh1")
                                cp("act", HBd[1], H[1], [("H", 1)], [HBKd[1]])
                        else:
                            if c == 0:
                                state_zero_bits(0)
                                P.dma("sp", H[0], init_d[j, 0][:, g * 512:(g + 1) * 512], writes=[("H", 0)], chan="ldh0")
                                cp("act", HBd[0], H[0], [("H", 0)], [HBKd[0]])
                            if c == 8:
                                memset("dve", HBd[0], 0.0, [HBKd[0]])
                        hbf = hdir_bf[d]
                        hbk = ("hdir_bf", d)
                        cp("act", hbf, HBd[d], [HBKd[d]], [hbk])
                        ob = PA[1][:, 0:512]
                        ok_ = PAK[1][0]
                        mm(ob, CgT[:, ck], hbf, True, True, ["CgT", hbk], [ok_])
                        tt("dve", yo[d].rearrange("p (h q) -> p h q", h=8), ob.rearrange("p (h q) -> p h q", h=8),
                           e_offk[:, d, c, hs].unsqueeze(2).to_broadcast([128, 8, 64]), ALU.mult, [ok_, "e_offk"], [("yo", d)])
                        if finalize:
                            par = c % 2
                            y_tm = y_tm2[par]
                            ytk = ("y_tm", par)
                            Yf = PA[0][:, (1 + par) * 512:(2 + par) * 512]
                            YfK = PAK[0][1 + par]
                            mm(Yf, ident, hb_bf[:, c, :], True, False, ["cb", ("stash", c)], [YfK])
                            mm(Yf, ident, yo[d], False, True, ["cb", ("yo", d)], [YfK])
                            cp("act", y_tm, Yf, [YfK], [ytk])
                            for t_ in range(4):
                                tr(PM[1][:, t_ * 128:(t_ + 1) * 128], y_tm[:, t_ * 128:(t_ + 1) * 128], [ytk], [PMK[1]], t_ == 0)
                            ygv = ygT[:, g * 4:(g + 1) * 4, ck]
                            gkeys = [("ygT", g * 4 + t_) for t_ in range(4)]
                            tt("dve", ygv, PM[1][:, 0:512].rearrange("p (t l) -> p t l", t=4), ygv, ALU.mult, [PMK[1]] + gkeys, gkeys)
                        else:
                            tt("dve", hb_bf[:, c, :], hb_bf[:, c, :], yo[d], ALU.add, [("stash", c), ("yo", d)], [("stash", c)])
                        state_update(d, c)
                        if d == 1 and c % 2 == 0:
                            state_out(1, c // 2)
                        if d == 0 and c % 2 == 1:
                            state_out(0, (c - 1) // 2)

                    for stp in range(NCH // 2):
                        diag_pass(stp)
                        diag_pass(NCH - 1 - stp)
                        sweep_step(1, NCH - 1 - stp, finalize=False)
                        sweep_step(0, stp, finalize=False)
                    for stp in range(NCH // 2, NCH):
                        sweep_step(1, NCH - 1 - stp, finalize=True)
                        sweep_step(0, stp, finalize=True)

                P.fill = None
                P.barrier()
                sq2 = [xraw.bitcast(BF16)[:, 0:T], cacc.bitcast(BF16)[:, 0:T]]
                rms_stats(lambda k: ygT[:, k, :], 16, sq2, 1.0 / 2048.0, lambda k: ("ygT", k))
                for k in range(16):
                    tsc("dve", ygT[:, k, :], ygT[:, k, :], normw[:, k:k + 1], ALU.mult, [("ygT", k), "sm"], [("ygT", k)])
                tmpo = [xraw, cacc]
                for ot in range(8):
                    a = ot % 2
                    wv, wk = ws.get(w_out[j][:, ot * 128:(ot + 1) * 128], 16, 128)
                    lin_tile(a, wv, wk, 0, lambda k: ygT[:, k, :], lambda k: ("ygT", k), [(k, k) for k in range(16)], True, True)
                    tt("dve", tmpo[a], PA[a][:, 0:T], rstd, ALU.mult, PAK[a] + ["rstd"], [("tmpo", a)])
                    resid_add(ot, tmpo[a], [("tmpo", a)], mb, mk, 16)

            a0, b0 = ada_steps(0)
            for s in a0:
                s()
            for i in range(n_layers):
                mb, mkA, mkB = mods[i % 2], ("modsA", i % 2), ("modsB", i % 2)
                if i % 2 == 0:
                    ssd(i // 2, i, mb, mkA, b0 if i == 0 else ())
                else:
                    fnet(i // 2, i, mb, mkA)
                inter = []
                if i + 1 < 4:
                    an, bn = ada_steps(i + 1)
                    inter = an + bn
                mlp(i, mb, mkB, inter)
            P.barrier()
            sqf = [view_at(PR0 + q * T * 2, T * 2, BF16) for q in range(2)]
            outf = [view_at(PR0 + 2 * T * 2 + q * T * 4, T * 4, F32) for q in range(2)]
            rms_stats(lambda k: x[:, k, :], 8, sqf, 1.0 / 1024.0, lambda k: ("x", k))
            for k in range(8):
                stt(outf[k % 2], x[:, k, :], sm[:, SM_FIN + k:SM_FIN + k + 1], rstd, ALU.mult, ALU.mult,
                    [("x", k), "sm", "rstd"], [("outf", k % 2)])
                P.dma("sp", yT_d[k * 128:(k + 1) * 128, :], outf[k % 2], reads=[("outf", k % 2)], chan=("out", k % 2))

        Pd = Prog(nc, dry=True)
        wsd = WStream(Pd, slots)
        gen(Pd, wsd)
        P = Prog(nc)
        ws = WStream(P, slots, plan=wsd.rec)
        gen(P, ws)
        assert ws.n == len(wsd.rec)
        P.emit()
        build_program.stats = dict(P.stats, est_us=getattr(P, "est_us", None), n_fill=P.n_fill)
    return nc


def _bf(a):
    return np.asarray(a, dtype=np.float32).astype(ml_dtypes.bfloat16)


def _consts():
    cbm = np.zeros((128, NCB), np.float32)
    cbm[:, CB_ID:CB_ID + 128] = np.eye(128)
    cbm[:, CB_ONES:CB_ONES + 128] = 1.0
    s = np.arange(128)[:, None]
    l = np.arange(128)[None, :]
    cbm[:, CB_TRIF:CB_TRIF + 128] = (s <= l)
    cbm[:, CB_TRIB:CB_TRIB + 128] = (s >= l)
    nf = np.where(s <= l, 0.0, -30000.0)
    nb = np.where(s >= l, 0.0, -30000.0)
    cbm[:, CB_NEGM:CB_NEGM + 512] = np.tile(nf, (1, 4))
    cbm[:, CB_NEGM + 512:CB_NEGM + 1024] = np.tile(nb, (1, 4))
    for jj in range(64):
        cbm[jj, CB_ECOL + jj] = 1.0
        cbm[64 + jj, CB_ECOL + jj] = 1.0
    d = np.arange(256)
    ang = 2.0 * np.pi * np.outer(d, d) / 256.0
    csd = np.concatenate([np.cos(ang), np.sin(ang)], axis=1)

    def dft(L):
        k = np.arange(L)
        a = 2.0 * np.pi * np.outer(k, k) / L
        sc = 1.0 / np.sqrt(L * 256.0)
        return np.stack([np.cos(a) * sc, -np.sin(a) * sc], axis=1)

    d256 = dft(256)
    d1024 = dft(1024)
    blk = np.zeros((1024, 2, 1024))
    for u in range(4):
        blk[u * 256:(u + 1) * 256, :, u * 256:(u + 1) * 256] = d256
    return _bf(cbm), _bf(csd), _bf(d1024), _bf(blk), _bf(d256)


def _col(v, ntiles):
    return np.asarray(v, np.float32).reshape(ntiles, 128).T


_NC_CACHE = {}


def kernel(x_prompt, x_sample, state_ssd, c, c_ctx, ada_w, ada_b, norm_mix_w, norm_mlp_w,
           ssd_w_in, ssd_conv_w, ssd_conv_b, ssd_dt_bias, ssd_a_log, ssd_d, ssd_norm_w,
           ssd_w_out, fno_w_out, fno_b_out, mlp_w1, mlp_w2, final_norm_w):
    f32 = lambda a: np.ascontiguousarray(np.asarray(a, dtype=np.float32))
    x_prompt, x_sample, state_ssd = f32(x_prompt), f32(x_sample), f32(state_ssd)
    c, c_ctx = f32(c), f32(c_ctx)
    cbm, csd, dft_dense, dft_blk, d256 = _consts()

    smalls = np.zeros((128, NSM), np.float32)
    for i in range(4):
        b = i * SM_LAYER
        smalls[:, b:b + 48] = _col(f32(ada_b)[i], 48)
        smalls[:, b + 48:b + 56] = _col(f32(norm_mix_w)[i], 8)
        smalls[:, b + 56:b + 64] = _col(f32(norm_mlp_w)[i], 8)
    for j in range(2):
        b = SM_SSD0 + j * 128
        cw = f32(ssd_conv_w)[j]
        cwt = np.stack([_col(cw[k], 24) for k in range(3)], axis=2)
        smalls[:, b:b + 72] = cwt.reshape(128, 72)
        smalls[:, b + 72:b + 96] = _col(f32(ssd_conv_b)[j], 24)
        smalls[:, b + 96:b + 112] = _col(f32(ssd_norm_w)[j], 16)
        smalls[:, SM_FNO0 + j * 8:SM_FNO0 + j * 8 + 8] = _col(f32(fno_b_out)[j], 8)
    smalls[:, SM_FIN:SM_FIN + 8] = _col(f32(final_norm_w), 8)

    weights = dict(ada_w=f32(ada_w), ssd_w_in=f32(ssd_w_in), ssd_w_out=f32(ssd_w_out), fno_w_out=f32(fno_w_out),
                   mlp_w1=f32(mlp_w1), mlp_w2=f32(mlp_w2))
    in_maps = []
    for core in range(8):
        rows = np.zeros((1, NR), np.float32)
        for j in range(2):
            rows[0, j * RW_SSD:j * RW_SSD + 64] = f32(ssd_dt_bias)[j].reshape(64)
            rows[0, j * RW_SSD + 64:j * RW_SSD + 128] = f32(ssd_a_log)[j].reshape(64)
            rows[0, j * RW_SSD + 128:j * RW_SSD + 160] = f32(ssd_d)[j]
        kf = np.zeros((2, 10), np.float32)
        ncut = np.zeros(19, np.float32)
        if core < 2:
            toks = np.concatenate([x_sample[core], x_prompt[30 + core]], axis=0)
            condA, condB = c[core], c_ctx
            init = np.stack([np.stack([state_ssd[core, j, d].transpose(2, 0, 1).reshape(128, 2048) for d in range(2)])
                             for j in range(2)])
            kf[0] = [1, 1, 1, 1, 1, 1, 1, 1, 1, 1]
            kf[1] = [1, 1, 1, 1, 1, 1, 1, 1, 1, 1]
            ncut[0:16] = -1.0
            dftA = dft_dense
        else:
            p0 = 5 * (core - 2)
            toks = np.concatenate([x_prompt[p0 + u] for u in range(5)], axis=0)
            condA, condB = c_ctx, c_ctx
            init = np.zeros((2, 2, 128, 2048), np.float32)
            kf[0] = [1, 1, 0, 1, 0, 1, 0, 1, 1, 1]
            kf[1] = [1, 0, 1, 0, 1, 0, 1, 1, 1, 1]
            for jb in (4, 8, 12, 16):
                ncut[jb - 1] = -1.0
            dftA = dft_blk
        rows[0, RW_KF:RW_KF + 20] = kf.reshape(20)
        rows[0, RW_CUT:RW_CUT + 19] = ncut
        cond = np.stack([_col(condA, 8), _col(condB, 8)], axis=2).reshape(128, 16)
        m = dict(xT=np.ascontiguousarray(toks.T), condT=np.ascontiguousarray(cond), smalls=smalls, rows=rows,
                 init_st=np.ascontiguousarray(init), cb=cbm, dftA=dftA, dftB=d256, csd=csd)
        m.update(weights)
        in_maps.append(m)

    if "nc" not in _NC_CACHE:
        _NC_CACHE["nc"] = build_program()
    nc = _NC_CACHE["nc"]
    res = run_bass_kernel_spmd(nc, in_maps, core_ids=list(range(8)))

    y_prompt = np.zeros((32, 256, 1024), np.float32)
    y_sample = np.zeros((2, 1024, 1024), np.float32)
    new_state = np.zeros((32, 2, 2, 32, 64, 128), np.float32)
    for core in range(8):
        r = res.results[core]
        y = np.asarray(r["yT"], np.float32).T
        so = np.asarray(r["st_out"], np.float32)

        def put_state(p, u):
            for j in range(2):
                for d in range(2):
                    new_state[p, j, d] = so[u, j, d].reshape(128, 32, 64).transpose(1, 2, 0)

        if core < 2:
            y_sample[core] = y[0:1024]
            y_prompt[30 + core] = y[1024:1280]
            put_state(30 + core, 4)
        else:
            p0 = 5 * (core - 2)
            for u in range(5):
                y_prompt[p0 + u] = y[u * 256:(u + 1) * 256]
                put_state(p0 + u, u)
    return (y_prompt, y_sample, new_state)
```

```python
import contextlib
import numpy as np
import ml_dtypes
import concourse.bass as bass
import concourse.mybir as mybir
from concourse.bass_utils import run_bass_kernel_spmd

F32 = mybir.dt.float32
BF16 = mybir.dt.bfloat16
AF = mybir.ActivationFunctionType
ALU = mybir.AluOpType

ENGS = ("pe", "act", "dve", "pool", "sp")
T = 1280
NCH = 10
EPS = 1e-6
NSLOT = 5
PRIO_CRITICAL_PATH = False
SCHED_WINDOW = 40
FILL_THR = 0.25
FILL_DUR = 0.25
FILL_MAX = 6
SLOT_ELEMS = 2048
RANGES = ((0, 512), (512, 1024), (1024, 1280))
URANGES = ((0, 0, 1024), (1, 1024, 1280))


class Prog:
    def __init__(self, nc, dry=False):
        self.nc = nc
        self.dry = dry
        self.ops = []
        self.last_w = {}
        self.readers = {}
        self.chan_last = {}
        self.chans = []
        self.phase = 0
        self.fill = None
        self.n_fill = 0

    def barrier(self):
        self.phase += 1

    def _add(self, eng, fn, reads, writes, chan=None, cost=0.5):
        if self.dry:
            return -1
        idx = len(self.ops)
        deps = set()
        odeps = set()
        for k in reads:
            w = self.last_w.get(k)
            if w is not None:
                deps.add(w)
        for k in writes:
            w = self.last_w.get(k)
            if w is not None:
                deps.add(w)
            for r in self.readers.get(k, ()):
                deps.add(r)
        if chan is not None:
            if chan not in self.chan_last:
                self.chans.append(chan)
            p = self.chan_last.get(chan)
            if p is not None:
                deps.add(p)
            self.chan_last[chan] = idx
        for k in writes:
            self.last_w[k] = idx
            self.readers[k] = []
        for k in reads:
            if k not in writes:
                lst = self.readers.setdefault(k, [])
                if chan is None:
                    for q, r in enumerate(lst):
                        if self.ops[r]["chan"] is None and self.ops[r]["eng"] == eng:
                            odeps.add(r)
                            lst[q] = idx
                            break
                    else:
                        lst.append(idx)
                else:
                    lst.append(idx)
        deps.discard(idx)
        odeps.discard(idx)
        odeps -= deps
        self.ops.append(dict(eng=eng, fn=fn, deps=sorted(deps), odeps=sorted(odeps), chan=chan, cost=cost,
                             phase=self.phase, fill=(self.fill if (eng == "pe" and chan is None) else None), gap=0.0))
        return idx

    def op(self, eng, fn, reads=(), writes=(), cost=0.5):
        return self._add(eng, fn, tuple(reads), tuple(writes), cost=cost)

    def dma(self, q, out, in_, reads=(), writes=(), chan="d0"):
        nbytes = 4.0
        for d in in_.shape:
            nbytes *= d
        return self._add(q, lambda e: e.dma_start(out=out, in_=in_), tuple(reads), tuple(writes), chan=chan,
                         cost=1.5 + nbytes / 300e3)

    def schedule(self, reorder=True):
        import heapq
        ops = self.ops
        n = len(ops)
        dependents = [[] for _ in range(n)]
        for i, o in enumerate(ops):
            for d in o["deps"]:
                dependents[d].append(i)
            for d in o["odeps"]:
                dependents[d].append(i)
        by_phase = {}
        for i, o in enumerate(ops):
            by_phase.setdefault(o["phase"], []).append(i)
        LAT = 0.1
        bl = [0.0] * n
        for i in range(n - 1, -1, -1):
            m = 0.0
            for j in dependents[i]:
                if bl[j] + LAT > m:
                    m = bl[j] + LAT
            bl[i] = ops[i]["cost"] + m
        if not PRIO_CRITICAL_PATH:
            bl = [0.0] * n
        t_e = {e: 0.0 for e in ENGS}
        fin_t = [None] * n
        order = []
        dma_free = 0.0
        last_sched_eng = {}
        last_sched_chan = {}
        for ph in sorted(by_phase):
            idxs = by_phase[ph]
            B = set(last_sched_eng.values())
            for ch, idx in last_sched_chan.items():
                if not (isinstance(ch, tuple) and ch[0] == "w"):
                    B.add(idx)
            nrem = {}
            ready_t = {}
            pending = {e: [] for e in ENGS}
            avail = {e: [] for e in ENGS}
            for i in idxs:
                o = ops[i]
                if B:
                    o["deps"] = sorted(set(o["deps"]) | B)
                r = 0.0
                cnt = 0
                for d in list(o["deps"]) + list(o["odeps"]):
                    if fin_t[d] is None:
                        cnt += 1
                    else:
                        r = max(r, fin_t[d] + LAT)
                nrem[i] = cnt
                ready_t[i] = r
                if cnt == 0:
                    heapq.heappush(pending[o["eng"]], (r if reorder else 0.0, -bl[i] if reorder else 0.0, i))
            done = 0
            sched_flag = {i: False for i in idxs}
            lo_pos = 0
            while done < len(idxs):
                best = None
                while lo_pos < len(idxs) and sched_flag[idxs[lo_pos]]:
                    lo_pos += 1
                lo_idx = idxs[lo_pos] if lo_pos < len(idxs) else 0
                for use_window in (True, False):
                  for e in ENGS:
                    pe_, av = pending[e], avail[e]
                    while pe_ and pe_[0][0] <= t_e[e]:
                        q_ = heapq.heappop(pe_)
                        heapq.heappush(av, (q_[1], q_[2]))
                    if av and use_window and av[0][1] > lo_idx + SCHED_WINDOW:
                        if pe_ and pe_[0][2] <= lo_idx + SCHED_WINDOW:
                            cand = (pe_[0][0], pe_[0][1], pe_[0][2], e, False)
                        else:
                            continue
                    elif av:
                        cand = (t_e[e], av[0][0], av[0][1], e, True)
                    elif pe_:
                        cand = (pe_[0][0], pe_[0][1], pe_[0][2], e, False)
                    else:
                        continue
                    key = cand[:3] if reorder else (cand[2],)
                    if best is None or key < best[0]:
                        best = (key, cand)
                  if best is not None:
                      break
                start, _pri, i, e, from_av = best[1]
                sched_flag[i] = True
                if from_av:
                    heapq.heappop(avail[e])
                else:
                    heapq.heappop(pending[e])
                o = ops[i]
                if e == "pe":
                    o["gap"] = max(0.0, start - t_e[e])
                start = max(start, t_e[e])
                if o["chan"] is not None:
                    t_e[e] = start + (1.0 if e == "pool" else 0.1)
                    fin = max(start, dma_free) + o["cost"]
                    dma_free = fin - 1.5
                    last_sched_chan[o["chan"]] = i
                else:
                    t_e[e] = start + o["cost"]
                    fin = t_e[e]
                    last_sched_eng[e] = i
                fin_t[i] = fin
                order.append(i)
                done += 1
                for j in dependents[i]:
                    if j in nrem:
                        if fin + LAT > ready_t[j]:
                            ready_t[j] = fin + LAT
                        nrem[j] -= 1
                        if nrem[j] == 0:
                            heapq.heappush(pending[ops[j]["eng"]], (ready_t[j] if reorder else 0.0, -bl[j] if reorder else 0.0, j))
        assert len(order) == n
        self.est_us = max(t_e.values())
        newidx = {old: new for new, old in enumerate(order)}
        new_ops = []
        for old in order:
            o = ops[old]
            o["deps"] = sorted(newidx[d] for d in o["deps"])
            o["odeps"] = sorted(newidx[d] for d in o["odeps"])
            assert all(d < newidx[old] for d in o["deps"]) and all(d < newidx[old] for d in o["odeps"])
            new_ops.append(o)
        self.ops = new_ops

    def emit(self, reorder=True):
        nc = self.nc
        self.schedule(reorder)
        ops = self.ops
        n = len(ops)

        def pe_pe(od, o):
            return od["chan"] is None and od["eng"] == "pe" and o["eng"] == "pe" and o["chan"] is None

        needed = [False] * n
        for i, o in enumerate(ops):
            for d in o["deps"]:
                if pe_pe(ops[d], o):
                    continue
                needed[d] = True
        last_of = {}
        for i, o in enumerate(ops):
            if o["chan"] is None:
                last_of[o["eng"]] = i
        for e, i in last_of.items():
            needed[i] = True
        cnt = {e: 0 for e in ENGS}
        ccnt = {c: 0 for c in self.chans}
        sig = [None] * n
        for i, o in enumerate(ops):
            if o["chan"] is not None:
                ccnt[o["chan"]] += 16
                sig[i] = (("c", o["chan"]), ccnt[o["chan"]])
            elif needed[i]:
                cnt[o["eng"]] += 1
                sig[i] = (("e", o["eng"]), cnt[o["eng"]])
        known = {e: {} for e in ENGS}
        waits = [None] * n
        for i, o in enumerate(ops):
            e = o["eng"]
            w = {}
            for d in o["deps"]:
                if pe_pe(ops[d], o):
                    continue
                sk, v = sig[d]
                if known[e].get(sk, 0) >= v:
                    continue
                if w.get(sk, 0) < v:
                    w[sk] = v
            for sk, v in w.items():
                known[e][sk] = v
            waits[i] = sorted(w.items(), key=lambda t: str(t[0]))
        final = {}
        for i, o in enumerate(ops):
            if sig[i] is not None:
                sk, v = sig[i]
                if final.get(sk, 0) < v:
                    final[sk] = v
        self.stats = dict(n_ops=n, per_eng={e: sum(1 for o in ops if o["eng"] == e) for e in ENGS},
                          n_sig=sum(1 for s in sig if s is not None), n_waits=sum(len(w) for w in waits),
                          maxcnt=dict(cnt))
        with contextlib.ExitStack() as st:
            sems = {}
            for e in ENGS:
                sems[("e", e)] = st.enter_context(nc.semaphore("s_" + e))
            for ci, c in enumerate(self.chans):
                sems[("c", c)] = st.enter_context(nc.semaphore("c_%d" % ci))
            block = st.enter_context(nc.Block())

            def run(engname):
                def body(eng):
                    for i, o in enumerate(ops):
                        if o["eng"] != engname:
                            continue
                        if o["fill"] is not None and o["gap"] > FILL_THR:
                            fo, fl, fr = o["fill"]
                            for _ in range(min(FILL_MAX, int(o["gap"] / FILL_DUR))):
                                eng.matmul(fo, fl, fr, start=True, stop=True, skip_group_check=True)
                                self.n_fill += 1
                        for sk, v in waits[i]:
                            eng.wait_ge(sems[sk], v)
                        ins = o["fn"](eng)
                        if sig[i] is not None:
                            sk, v = sig[i]
                            ins.then_inc(sems[sk], 16 if sk[0] == "c" else 1)
                    if engname == "sp":
                        for sk, v in sorted(final.items(), key=lambda t: str(t[0])):
                            eng.wait_ge(sems[sk], v)
                return body

            block.tensor(run("pe"))
            block.scalar(run("act"))
            block.vector(run("dve"))
            block.gpsimd(run("pool"))
            block.sync(run("sp"))


class WStream:
    def __init__(self, P, slots, plan=None):
        self.P = P
        self.slots = slots
        self.plan = plan
        self.rec = []
        self.n = 0
        self.issued = 0

    def view(self, i, kt, ncols):
        return self.slots[:, i % NSLOT, 0:kt * ncols].rearrange("p (k n) -> p k n", k=kt)

    def get(self, dram_ap, kt, ncols):
        i = self.n
        self.n += 1
        self.rec.append((dram_ap, kt, ncols))
        if self.plan is not None:
            while self.issued < min(len(self.plan), i + NSLOT):
                q = self.issued
                ap, k2, n2 = self.plan[q]
                self.P.dma("pool", self.view(q, k2, n2), ap.rearrange("(k p) n -> p k n", p=128),
                           writes=[("slot", q % NSLOT)], chan=("w", q % NSLOT))
                self.issued += 1
        return self.view(i, kt, ncols), ("slot", i % NSLOT)


SM_LAYER = 64
SM_SSD0 = 4 * SM_LAYER
SM_FNO0 = SM_SSD0 + 2 * 128
SM_FIN = SM_FNO0 + 16
NSM = SM_FIN + 8
RW_SSD = 160
RW_KF = 2 * RW_SSD
RW_CUT = RW_KF + 20
NR = RW_CUT + 20
CB_ID, CB_ONES, CB_TRIF, CB_TRIB, CB_NEGM, CB_ECOL, CB_ZERO = 0, 128, 256, 384, 512, 1536, 1600
NCB = 1728


def build_program(n_layers=4, dbg=False):
    nc = bass.Bass("TRN2", target_bir_lowering=False)
    dr = lambda name, shape, dt: nc.dram_tensor(name, shape, dt, kind="ExternalInput").ap()
    xT_d = dr("xT", [1024, T], F32)
    cond_d = dr("condT", [128, 16], F32)
    sm_d = dr("smalls", [128, NSM], F32)
    rows_d = dr("rows", [1, NR], F32)
    init_d = dr("init_st", [2, 2, 128, 2048], F32)
    cb_d = dr("cb", [128, NCB], BF16)
    dftA_d = dr("dftA", [1024, 2, 1024], BF16)
    dftB_d = dr("dftB", [256, 2, 256], BF16)
    csd_d = dr("csd", [256, 512], BF16)
    ada_w = dr("ada_w", [4, 1024, 6144], F32)
    w_in = dr("ssd_w_in", [2, 1024, 5184], F32)
    w_out = dr("ssd_w_out", [2, 2048, 1024], F32)
    fno_w = dr("fno_w_out", [2, 1024, 1024], F32)
    w1 = dr("mlp_w1", [4, 1024, 4096], F32)
    w2 = dr("mlp_w2", [4, 4096, 1024], F32)
    yT_d = nc.dram_tensor("yT", [1024, T], F32, kind="ExternalOutput").ap()
    st_d = nc.dram_tensor("st_out", [5, 2, 2, 128, 2048], F32, kind="ExternalOutput").ap()

    with contextlib.ExitStack() as st:
        ARENA = 209920
        arena = st.enter_context(nc.sbuf_tensor("arena", [128, ARENA // 2], BF16))
        PA = [st.enter_context(nc.psum_tensor("pa%d" % i, [128, 1536], F32)) for i in range(2)]
        PM = [st.enter_context(nc.psum_tensor("pm%d" % i, [128, 512], F32)) for i in range(2)]

        off = [0]

        def alloc(nbytes, dt, pat=None, **kw):
            o = off[0]
            nb = (nbytes + 63) // 64 * 64
            off[0] = o + nb
            assert off[0] <= ARENA, ("arena overflow", off[0])
            return view_at(o, nbytes, dt, pat, **kw)

        def view_at(o, nbytes, dt, pat=None, **kw):
            v = arena[:, o // 2:(o + nbytes) // 2]
            if dt == F32:
                v = v.bitcast(F32)
            if pat:
                v = v.rearrange(pat, **kw)
            return v

        x = alloc(8 * T * 4, F32, "p (k t) -> p k t", k=8)
        hT = alloc(8 * T * 2, BF16, "p (k t) -> p k t", k=8)
        slots = alloc(NSLOT * SLOT_ELEMS * 2, BF16, "p (s e) -> p s e", s=NSLOT)
        cb = alloc(NCB * 2, BF16)
        sm = alloc(NSM * 4, F32)
        rows = alloc(NR * 4, F32)
        condT = alloc(16 * 4, F32, "p (k u) -> p k u", u=2)
        scond = alloc(16 * 2, BF16, "p (k u) -> p k u", u=2)
        mods = [alloc(64 * 2 * 4, F32, "p (m u) -> p m u", u=2) for _ in range(2)]
        rstd = alloc(T * 4, F32)
        PR0 = off[0]
        PR_BYTES = ARENA - PR0

        ident = cb[:, CB_ID:CB_ID + 128]
        ones_b = cb[:, CB_ONES:CB_ONES + 128]
        tri = [cb[:, CB_TRIF:CB_TRIF + 128], cb[:, CB_TRIB:CB_TRIB + 128]]
        negm = [cb[:, CB_NEGM:CB_NEGM + 512], cb[:, CB_NEGM + 512:CB_NEGM + 1024]]
        ecol = cb[:, CB_ECOL:CB_ECOL + 64]
        zeros_b = cb[:, CB_ZERO:CB_ZERO + 128]

        def pbank(acc, b):
            return ("pa", acc, b)

        PAK = [[pbank(a, b) for b in range(3)] for a in range(2)]
        PMK = [("pm", 0), ("pm", 1)]

        def gen(P, ws):
            def fsz(ap):
                n_ = 1
                for d_ in ap.shape[1:]:
                    n_ *= d_
                return n_

            def mm(out, lhsT, rhs, start, stop, reads, writes):
                P.op("pe", lambda e: e.matmul(out, lhsT, rhs, start=start, stop=stop, skip_group_check=True), reads, writes,
                     cost=0.06 + fsz(out) / 2400.0)

            def tr(out, in_, reads, writes, start):
                P.op("pe", lambda e: e.matmul(out, in_, ident, start=start, stop=True, skip_group_check=True),
                     tuple(reads) + ("cb",), writes, cost=0.12)

            def act(out, in_, func, reads, writes, bias=None, scale=None):
                kw = {}
                if bias is not None:
                    kw["bias"] = bias
                if scale is not None:
                    kw["scale"] = scale
                P.op("act", lambda e: e.activation(out=out, in_=in_, func=func, **kw), reads, writes,
                     cost=0.22 + fsz(out) / 1200.0)

            def vcost(eng, out):
                return (0.12 + fsz(out) / 900.0) * (2.0 if eng == "pool" else 1.0)

            def tt(eng, out, in0, in1, op, reads, writes):
                P.op(eng, lambda e: e.tensor_tensor(out=out, in0=in0, in1=in1, op=op), reads, writes, cost=vcost(eng, out))

            def stt(out, in0, scalar, in1, op0, op1, reads, writes):
                P.op("dve", lambda e: e.scalar_tensor_tensor(out=out, in0=in0, scalar=scalar, in1=in1, op0=op0, op1=op1), reads, writes,
                     cost=vcost("dve", out))

            def cp(eng, out, in_, reads, writes):
                if eng == "act":
                    P.op("act", lambda e: e.activation(out=out, in_=in_, func=AF.Copy), reads, writes,
                         cost=0.22 + fsz(out) / 1200.0)
                else:
                    P.op(eng, lambda e: e.tensor_copy(out=out, in_=in_), reads, writes, cost=vcost(eng, out))

            def tsc(eng, out, in0, s1, op0, reads, writes):
                P.op(eng, lambda e: e.tensor_scalar(out=out, in0=in0, scalar1=s1, scalar2=None, op0=op0), reads, writes,
                     cost=vcost(eng, out))

            def memset(eng, out, val, writes):
                P.op(eng, lambda e: e.memset(out, val), (), writes, cost=vcost(eng, out))

            P.dma("sp", cb, cb_d, writes=["cb"], chan="ld0")
            P.dma("sp", sm, sm_d, writes=["sm"], chan="ld1")
            P.dma("sp", rows, rows_d.partition_broadcast(128), writes=["rows"], chan="ld2")
            P.dma("sp", condT.rearrange("p k u -> p (k u)"), cond_d, writes=["cond"], chan="ld3")
            for k in range(8):
                P.dma("sp", x[:, k, :], xT_d[k * 128:(k + 1) * 128, :], writes=[("x", k)], chan=("ldx", k % 4))
            act(scond, condT, AF.Silu, ["cond"], ["scond"])

            def ada_steps(i):
                mb = mods[i % 2]
                halves = []
                for half in range(2):
                    mk = ("modsA" if half == 0 else "modsB", i % 2)
                    steps = []
                    for pq in range(12):
                        def step(pq=pq, half=half, mk=mk):
                            pc = half * 12 + pq
                            wv, wk = ws.get(ada_w[i][:, pc * 256:(pc + 1) * 256], 8, 256)
                            for o in range(2):
                                ot = pc * 2 + o
                                for k in range(8):
                                    mm(PM[0][:, ot * 2:ot * 2 + 2], wv[:, k, o * 128:(o + 1) * 128], scond[:, k, :],
                                       start=(pq == 0 and o == 0 and k == 0), stop=(pq == 11 and o == 1 and k == 7),
                                       reads=[wk, "scond"], writes=[PMK[0]])
                            if pq == 11:
                                c0 = half * 24
                                adab = sm[:, i * SM_LAYER + c0:i * SM_LAYER + c0 + 24]
                                tt("dve", mb[:, c0:c0 + 24, :], PM[0][:, 2 * c0:2 * c0 + 48].rearrange("p (m u) -> p m u", u=2),
                                   adab.unsqueeze(2).to_broadcast([128, 24, 2]), ALU.add, [PMK[0], "sm"], [mk])
                                nw = sm[:, i * SM_LAYER + 48 + 8 * half:i * SM_LAYER + 56 + 8 * half]
                                stt(mb[:, 48 + 8 * half:56 + 8 * half, :], mb[:, 8 + 24 * half:16 + 24 * half, :], 1.0,
                                    nw.unsqueeze(2).to_broadcast([128, 8, 2]), ALU.add, ALU.mult, [mk, "sm"], [mk])
                        steps.append(step)
                    halves.append(steps)
                return halves

            def rms_stats(src_tiles, nk, sqbuf, inv_n, src_keys, sq_func_in_bf16=False):
                for k in range(nk):
                    sq = sqbuf[k % 2]
                    sqk = ("sq", k % 2)
                    act(sq, src_tiles(k), AF.Square, [src_keys(k)], [sqk])
                    for r, (r0, r1) in enumerate(RANGES):
                        mm(PA[0][:, r0:r1], ones_b, sq[:, r0:r1], start=(k == 0), stop=(k == nk - 1),
                           reads=[sqk, "cb"], writes=[PAK[0][r]])
                act(rstd, PA[0][:, 0:T], AF.Sqrt, PAK[0], ["rstd"], bias=EPS, scale=inv_n)
                P.op("dve", lambda e: e.reciprocal(out=rstd, in_=rstd), ["rstd"], ["rstd"], cost=1.5)

            def norm_mod(mb, mk, a_off, b_off, sqbuf, tmpf):
                rms_stats(lambda k: x[:, k, :], 8, sqbuf, 1.0 / 1024.0, lambda k: ("x", k))
                for k in range(8):
                    tf = tmpf[k % 2]
                    tk = ("tmpf", k % 2)
                    for u, r0, r1 in URANGES:
                        stt(tf[:, r0:r1], x[:, k, r0:r1], mb[:, a_off + k, u:u + 1], rstd[:, r0:r1], ALU.mult, ALU.mult,
                            [("x", k), mk, "rstd"], [tk])
                    for u, r0, r1 in URANGES:
                        act(hT[:, k, r0:r1], tf[:, r0:r1], AF.Identity, [tk, mk], [("hT", k)], bias=mb[:, b_off + k, u:u + 1])

            def resid_add(ot, src, src_keys, mb, mk, g_off):
                for u, r0, r1 in URANGES:
                    stt(x[:, ot, r0:r1], src[:, r0:r1], mb[:, g_off + ot, u:u + 1], x[:, ot, r0:r1], ALU.mult, ALU.add,
                        list(src_keys) + [mk, ("x", ot)], [("x", ot)])

            def lin_tile(acc, wv, wk, col0, in_tile, in_key, k_list, first, last):
                for ki, (kslot, kin) in enumerate(k_list):
                    for r, (r0, r1) in enumerate(RANGES):
                        mm(PA[acc][:, r0:r1], wv[:, kslot, col0:col0 + 128], in_tile(kin)[:, r0:r1],
                           start=(first and ki == 0), stop=(last and ki == len(k_list) - 1),
                           reads=[wk, in_key(kin)], writes=[PAK[acc][r]])

            def mlp(i, mb, mk, inter_steps):
                o = PR0
                h1T = view_at(o, 32 * T * 2, BF16, "p (k t) -> p k t", k=32)
                o += 32 * T * 2
                rt = [view_at(o + q * T * 4, T * 4, F32) for q in range(2)]
                sqbuf = [view_at(PR0 + q * T * 2, T * 2, BF16) for q in range(2)]
                P.barrier()
                norm_mod(mb, mk, 56, 24, sqbuf, rt)
                inter = list(inter_steps)
                for pc in range(16):
                    wv, wk = ws.get(w1[i][:, pc * 256:(pc + 1) * 256], 8, 256)
                    for oo in range(2):
                        ft = pc * 2 + oo
                        a = ft % 2
                        lin_tile(a, wv, wk, oo * 128, lambda k: hT[:, k, :], lambda k: ("hT", k),
                                 [(k, k) for k in range(8)], True, True)
                        act(rt[a], PA[a][:, 0:T], AF.Relu, PAK[a], [("rt", a)])
                        tt("dve", h1T[:, ft, :], rt[a], rt[a], ALU.mult, [("rt", a)], [("h1T", ft)])
                    if inter:
                        inter.pop(0)()
                for ot in range(8):
                    a = ot % 2
                    for half in range(2):
                        wv, wk = ws.get(w2[i][half * 2048:(half + 1) * 2048, ot * 128:(ot + 1) * 128], 16, 128)
                        lin_tile(a, wv, wk, 0, lambda k: h1T[:, k, :], lambda k: ("h1T", k),
                                 [(kk, half * 16 + kk) for kk in range(16)], half == 0, half == 1)
                        if inter:
                            inter.pop(0)()
                    resid_add(ot, PA[a], PAK[a], mb, mk, 40)
                while inter:
                    inter.pop(0)()

            def fnet(j, i, mb, mk):
                o = PR0
                Y = view_at(o, 10 * 4 * 512 * 2, BF16, "p (t g c) -> p t g c", t=10, g=4)
                sqbuf = [view_at(o + q * T * 2, T * 2, BF16) for q in range(2)]
                o += 10 * 4 * 512 * 2
                fT = view_at(o, 8 * T * 2, BF16, "p (k t) -> p k t", k=8)
                o += 8 * T * 2
                dA_s = view_at(o, 8 * 2 * 1024 * 2, BF16, "p (l c k) -> p l c k", l=8, c=2)
                o += 8 * 2 * 1024 * 2
                dB_s = view_at(o, 2 * 2 * 256 * 2, BF16, "p (l c k) -> p l c k", l=2, c=2)
                o += 2 * 2 * 256 * 2
                csd_s = view_at(o, 2 * 512 * 2, BF16, "p (l c) -> p l c", l=2)
                o += 2 * 512 * 2
                assert o <= ARENA, o
                tmpf = [view_at(PR0 + 2 * T * 2 + q * T * 4, T * 4, F32) for q in range(2)]
                P.barrier()
                for l in range(8):
                    P.dma("sp", dA_s[:, l], dftA_d[l * 128:(l + 1) * 128], writes=[("dftA", l)], chan=("ldf", l % 4))
                for l in range(2):
                    P.dma("sp", dB_s[:, l], dftB_d[l * 128:(l + 1) * 128], writes=["dftB"], chan=("ldf", l))
                    P.dma("sp", csd_s[:, l], csd_d[l * 128:(l + 1) * 128], writes=["csd"], chan=("ldf", 2 + l))
                norm_mod(mb, mk, 48, 0, sqbuf, tmpf)
                P.barrier()
                n = 0
                for tt_ in range(10):
                    for gc in range(4):
                        b = n % 2
                        for kt in range(2):
                            mm(PM[b][:, 0:512], hT[:, 2 * gc + kt, tt_ * 128:(tt_ + 1) * 128], csd_s[:, kt, :],
                               start=(kt == 0), stop=(kt == 1), reads=[("hT", 2 * gc + kt), "csd"], writes=[PMK[b]])
                        cp("act" if n % 2 == 0 else "dve", Y[:, tt_, gc, :], PM[b][:, 0:512], [PMK[b]], [("Y", tt_)])
                        n += 1
                for e_ in range(8):
                    gc, half = e_ // 2, e_ % 2
                    a = e_ % 2
                    for r in range(2):
                        cnt = 0
                        for lt in range(8):
                            for cs_ in range(2):
                                mm(PA[a][:, r * 512:(r + 1) * 512], Y[:, lt, gc, cs_ * 256 + half * 128:cs_ * 256 + half * 128 + 128],
                                   dA_s[:, lt, cs_, r * 512:(r + 1) * 512], start=(cnt == 0), stop=(cnt == 15),
                                   reads=[("Y", lt), ("dftA", lt)], writes=[PAK[a][r]])
                                cnt += 1
                    cnt = 0
                    for lt in range(8, 10):
                        for cs_ in range(2):
                            mm(PA[a][:, 1024:1280], Y[:, lt, gc, cs_ * 256 + half * 128:cs_ * 256 + half * 128 + 128],
                               dB_s[:, lt - 8, cs_, :], start=(cnt == 0), stop=(cnt == 3),
                               reads=[("Y", lt), "dftB"], writes=[PAK[a][2]])
                            cnt += 1
                    cp("act" if e_ % 2 == 0 else "dve", fT[:, e_, :], PA[a][:, 0:T], PAK[a], [("fT", e_)])
                P.barrier()
                for pc in range(4):
                    wv, wk = ws.get(fno_w[j][:, pc * 256:(pc + 1) * 256], 8, 256)
                    for oo in range(2):
                        ot = pc * 2 + oo
                        a = ot % 2
                        lin_tile(a, wv, wk, oo * 128, lambda k: fT[:, k, :], lambda k: ("fT", k),
                                 [(k, k) for k in range(8)], True, True)
                        act(tmpf[a], PA[a][:, 0:T], AF.Identity, PAK[a] + ["sm"], [("tmpf", a)],
                            bias=sm[:, SM_FNO0 + j * 8 + ot:SM_FNO0 + j * 8 + ot + 1])
                        resid_add(ot, tmpf[a], [("tmpf", a)], mb, mk, 16)

            def ssd(j, i, mb, mk, inter_z=()):
                smb = SM_SSD0 + j * 128
                convw = sm[:, smb:smb + 72].rearrange("p (t k) -> p t k", k=3)
                convb = sm[:, smb + 72:smb + 96]
                normw = sm[:, smb + 96:smb + 112]
                rwb = j * RW_SSD
                dtb_bc = rows[:, rwb:rwb + 64]
                alog_bc = rows[:, rwb + 64:rwb + 128]
                d_bc = rows[:, rwb + 128:rwb + 160]
                kf_bc = rows[:, RW_KF:RW_KF + 20].rearrange("p (d c) -> p d c", d=2)
                ncut_bc = rows[:, RW_CUT:RW_CUT + 19]
                wj = w_in[j]

                o = [PR0]

                def al(nbytes, dt, pat=None, **kw):
                    v = view_at(o[0], nbytes, dt, pat, **kw)
                    o[0] += (nbytes + 63) // 64 * 64
                    assert o[0] <= ARENA, ("ssd region overflow", o[0])
                    return v

                ygT = al(16 * T * 2, BF16, "p (k t) -> p k t", k=16)
                BgT = al(T * 2, BF16)
                CgT = al(T * 2, BF16)
                xTt = [al(T * 2, BF16) for _ in range(2)]
                x_tm = al(10 * 512 * 2, BF16, "p (c f) -> p c f", c=10)
                B_tm = al(10 * 128 * 2, BF16, "p (c f) -> p c f", c=10)
                CBT = al(10 * 128 * 2, BF16, "p (c f) -> p c f", c=10)
                hb_bf = al(10 * 512 * 2, BF16, "p (c f) -> p c f", c=10)
                hf_bf = al(512 * 2, BF16)
                LT2 = al(512 * 2, BF16)
                H = [al(512 * 4, F32) for _ in range(2)]
                LT = [al(512 * 2, BF16) for _ in range(2)]
                MT = [al(512 * 2, BF16) for _ in range(2)]
                yo = [al(512 * 2, BF16) for _ in range(2)]
                xwt = [al(512 * 2, BF16) for _ in range(2)]
                xD = al(512 * 2, BF16)
                y_tm2 = [al(512 * 2, BF16) for _ in range(2)]
                cv1 = al(T * 4, F32)
                e_offk = al(640 * 4, F32, "p (d c h) -> p d c h", d=2, c=10)
                wst = al(640 * 4, F32, "p (d c h) -> p d c h", d=2, c=10)
                kcd = al(640 * 4, F32, "p (d c h) -> p d c h", d=2, c=10)
                cshlT = al(T * 2, BF16)
                nuhlT = al(T * 2, BF16)
                corr = al(2 * 2 * 20 * 4, F32, "p (q a b) -> p q a b", q=2, a=2)
                hb_off = PR0 + 16 * T * 2 + 4 * ((T * 2 + 63) // 64 * 64) + 10 * 512 * 2 + 2 * 10 * 128 * 2
                xraw = view_at(hb_off, T * 4, F32)
                cacc = view_at(hb_off + T * 4, T * 4, F32)
                so = [PR0 + 16 * T * 2 + 4 * ((T * 2 + 63) // 64 * 64)]

                def sal(nbytes, dt, pat=None, **kw):
                    v = view_at(so[0], nbytes, dt, pat, **kw)
                    so[0] += (nbytes + 63) // 64 * 64
                    return v

                dtv = sal(640 * 4, F32, "p (c j) -> p c j", c=10)
                dA = sal(640 * 4, F32, "p (c j) -> p c j", c=10)
                dAh = sal(640 * 2, BF16, "p (c j) -> p c j", c=10)
                dAl = sal(640 * 2, BF16, "p (c j) -> p c j", c=10)
                A_bc = sal(64 * 4, F32)
                cs = sal(640 * 4, F32, "p (d c h) -> p d c h", d=2, c=10)
                tot = sal(640 * 4, F32, "p (d c h) -> p d c h", d=2, c=10)
                t1 = sal(640 * 4, F32, "p (d c h) -> p d c h", d=2, c=10)
                negu = sal(640 * 4, F32, "p (d c h) -> p d c h", d=2, c=10)
                cshl = sal(10 * 128 * 2, BF16, "p (c f) -> p c f", c=10)
                nuhl = sal(10 * 128 * 2, BF16, "p (c f) -> p c f", c=10)
                assert so[0] <= PR0 + 16 * T * 2 + 4 * ((T * 2 + 63) // 64 * 64) + 10 * 512 * 2 * 2 + 2 * 10 * 128 * 2 + 1024, so[0]
                dt_dch = dtv.rearrange("p c (d h) -> p d c h", d=2)

                sqbuf = [xraw.bitcast(BF16)[:, 0:T], cacc.bitcast(BF16)[:, 0:T]]
                P.barrier()
                norm_mod(mb, mk, 48, 0, sqbuf, [xraw, cacc])
                P.barrier()

                wv, wk = ws.get(wj[:, 5120:5184], 8, 64)
                for c in range(NCH):
                    bnk = 0 if c < 8 else 1
                    cc = c if c < 8 else c - 8
                    for k in range(8):
                        mm(PM[bnk][:, cc * 64:(cc + 1) * 64], hT[:, k, c * 128:(c + 1) * 128], wv[:, k, 0:64],
                           start=(cc == 0 and k == 0), stop=(k == 7), reads=[wk, ("hT", k)], writes=[PMK[bnk]])
                tt("dve", dtv[:, 0:8, :], PM[0][:, 0:512].rearrange("p (c j) -> p c j", c=8),
                   dtb_bc.unsqueeze(1).to_broadcast([128, 8, 64]), ALU.add, [PMK[0], "rows"], ["dtv"])
                tt("dve", dtv[:, 8:10, :], PM[1][:, 0:128].rearrange("p (c j) -> p c j", c=2),
                   dtb_bc.unsqueeze(1).to_broadcast([128, 2, 64]), ALU.add, [PMK[1], "rows"], ["dtv"])
                act(dtv, dtv, AF.Exp, ["dtv"], ["dtv"])
                act(dtv, dtv, AF.Ln, ["dtv"], ["dtv"], bias=1.0)
                act(A_bc, alog_bc, AF.Exp, ["rows"], ["A_bc"])
                stt(dA, dtv, -1.0, A_bc.unsqueeze(1).to_broadcast([128, 10, 64]), ALU.mult, ALU.mult, ["dtv", "A_bc"], ["dA"])
                cp("act", dAh, dA, ["dA"], ["dAh"])
                tt("dve", dAl, dA, dAh, ALU.subtract, ["dA", "dAh"], ["dAl"])
                for d in range(2):
                    outv = PM[d][:, 0:320].rearrange("p (c h) -> p c h", c=10)
                    mm(outv, tri[d], dAh[:, :, d * 32:(d + 1) * 32], True, False, ["dAh", "cb"], [PMK[d]])
                    mm(outv, tri[d], dAl[:, :, d * 32:(d + 1) * 32], False, True, ["dAl", "cb"], [PMK[d]])
                    cp("act", cs[:, d], outv, [PMK[d]], ["cs"])
                    mm(outv, ones_b, dAh[:, :, d * 32:(d + 1) * 32], True, False, ["dAh", "cb"], [PMK[d]])
                    mm(outv, ones_b, dAl[:, :, d * 32:(d + 1) * 32], False, True, ["dAl", "cb"], [PMK[d]])
                    cp("dve", tot[:, d], outv, [PMK[d]], ["tot"])
                kfb = kf_bc.unsqueeze(3).to_broadcast([128, 2, 10, 32])
                act(e_offk, cs, AF.Exp, ["cs"], ["e_offk"])
                tt("dve", e_offk, e_offk, kfb, ALU.mult, ["e_offk", "rows"], ["e_offk"])
                tt("dve", t1, tot, cs, ALU.subtract, ["tot", "cs"], ["t1"])
                act(t1, t1, AF.Exp, ["t1"], ["t1"])
                tt("dve", wst, t1, dt_dch, ALU.mult, ["t1", "dtv"], ["wst"])
                act(kcd, tot, AF.Exp, ["tot"], ["kcd"])
                tt("dve", kcd, kcd, kfb, ALU.mult, ["kcd", "rows"], ["kcd"])
                act(t1, dt_dch, AF.Ln, ["dtv", "wst"], ["t1"])
                tt("dve", negu, t1, cs, ALU.subtract, ["t1", "cs"], ["negu"])
                for src, skey, dst, dkey in ((cs, "cs", cshl, "cshl"), (negu, "negu", nuhl, "nuhl")):
                    dv = dst.rearrange("p c (hl d h) -> p hl d c h", hl=2, d=2)
                    cp("act", dv[:, 0], src, [skey], [dkey])
                    tt("dve", dv[:, 1], src, dv[:, 0], ALU.subtract, [skey, dkey], [dkey])
                for src, skey, dstT, dkey in ((cshl, "cshl", cshlT, "cshlT"), (nuhl, "nuhl", nuhlT, "nuhlT")):
                    for c0 in range(0, NCH, 4):
                        n4 = min(4, NCH - c0)
                        for q in range(n4):
                            tr(PM[0][:, q * 128:(q + 1) * 128], src[:, c0 + q, :], [skey], [PMK[0]], q == 0)
                        cp("dve", dstT[:, c0 * 128:(c0 + n4) * 128], PM[0][:, 0:n4 * 128], [PMK[0]], [dkey])

                inter_z = list(inter_z)
                for pc in range(8):
                    for _ in range(2):
                        if inter_z:
                            inter_z.pop(0)()
                    wv, wk = ws.get(wj[:, pc * 256:(pc + 1) * 256], 8, 256)
                    for oo in range(2):
                        zt = pc * 2 + oo
                        a = zt % 2
                        lin_tile(a, wv, wk, oo * 128, lambda k: hT[:, k, :], lambda k: ("hT", k),
                                 [(k, k) for k in range(8)], True, True)
                        act(ygT[:, zt, :], PA[a][:, 0:T], AF.Silu, PAK[a], [("ygT", zt)])

                P.barrier()
                cbuf = [rstd, cv1]
                conv_n = [0]

                def conv_tile(a, ct, out, okey):
                    q = conv_n[0] % 2
                    conv_n[0] += 1
                    cb_ = cbuf[q]
                    ck_ = ("cacc", q)
                    crk = ("corr", q)
                    ca3 = cb_.rearrange("p (s w) -> p s w", w=64)
                    pa = PA[a][:, 0:T]
                    pa3 = pa.rearrange("p (s w) -> p s w", w=64)
                    pk = PAK[a]
                    act(cb_, pa, AF.Identity, pk + ["sm"], [ck_], bias=convb[:, ct:ct + 1], scale=convw[:, ct, 1:2])
                    stt(cb_[:, 1:T], PA[a][:, 0:T - 1], convw[:, ct, 0:1], cb_[:, 1:T], ALU.mult, ALU.add, pk + [ck_, "sm"], [ck_])
                    stt(cb_[:, 0:T - 1], PA[a][:, 1:T], convw[:, ct, 2:3], cb_[:, 0:T - 1], ALU.mult, ALU.add, pk + [ck_, "sm"], [ck_])
                    tt("dve", corr[:, q, 0, 0:19], pa3[:, 0:19, 63], ncut_bc, ALU.mult, pk + ["rows"], [crk])
                    tt("dve", corr[:, q, 1, 0:19], pa3[:, 1:20, 0], ncut_bc, ALU.mult, pk + ["rows"], [crk])
                    stt(ca3[:, 1:20, 0], corr[:, q, 0, 0:19], convw[:, ct, 0:1], ca3[:, 1:20, 0], ALU.mult, ALU.add, [crk, ck_, "sm"], [ck_])
                    stt(ca3[:, 0:19, 63], corr[:, q, 1, 0:19], convw[:, ct, 2:3], ca3[:, 0:19, 63], ALU.mult, ALU.add, [crk, ck_, "sm"], [ck_])
                    act(out, cb_, AF.Silu, [ck_], [okey])

                tr_n = [0]

                def transposes_to_tm(srcT, skey, dst3, dkey, f0):
                    for c0 in range(0, NCH, 4):
                        n4 = min(4, NCH - c0)
                        pb = tr_n[0] % 2
                        tr_n[0] += 1
                        for q in range(n4):
                            tr(PM[pb][:, q * 128:(q + 1) * 128], srcT[:, (c0 + q) * 128:(c0 + q + 1) * 128], [skey], [PMK[pb]], q == 0)
                        cp("act", dst3[:, c0:c0 + n4, f0:f0 + 128], PM[pb][:, 0:n4 * 128].rearrange("p (c f) -> p c f", c=n4),
                           [PMK[pb]], [dkey])

                for g in range(4):
                    for pc in range(2):
                        wv, wk = ws.get(wj[:, 2048 + g * 512 + pc * 256:2048 + g * 512 + (pc + 1) * 256], 8, 256)
                        for oo in range(2):
                            t_ = pc * 2 + oo
                            a = t_ % 2
                            lin_tile(a, wv, wk, oo * 128, lambda k: hT[:, k, :], lambda k: ("hT", k),
                                     [(k, k) for k in range(8)], True, True)
                            conv_tile(a, g * 4 + t_, xTt[a], ("xTt", a))
                            transposes_to_tm(xTt[a], ("xTt", a), x_tm, "x_tm", t_ * 128)
                    wv, wk = ws.get(wj[:, 4096 + g * 128:4096 + (g + 1) * 128], 8, 128)
                    lin_tile(0, wv, wk, 0, lambda k: hT[:, k, :], lambda k: ("hT", k), [(k, k) for k in range(8)], True, True)
                    conv_tile(0, 16 + g, BgT, "BgT")
                    transposes_to_tm(BgT, "BgT", B_tm, "B_tm", 0)
                    wv, wk = ws.get(wj[:, 4608 + g * 128:4608 + (g + 1) * 128], 8, 128)
                    lin_tile(1, wv, wk, 0, lambda k: hT[:, k, :], lambda k: ("hT", k), [(k, k) for k in range(8)], True, True)
                    conv_tile(1, 20 + g, CgT, "CgT")
                    for c0 in range(0, NCH, 4):
                        n4 = min(4, NCH - c0)
                        for q in range(n4):
                            c = c0 + q
                            mm(PM[0][:, q * 128:(q + 1) * 128], BgT[:, c * 128:(c + 1) * 128], CgT[:, c * 128:(c + 1) * 128],
                               start=(q == 0), stop=(q == n4 - 1), reads=["BgT", "CgT"], writes=[PMK[0]])
                        cp("act", CBT[:, c0:c0 + n4, :], PM[0][:, 0:n4 * 128].rearrange("p (c f) -> p c f", c=n4), [PMK[0]], ["CBT"])

                    hs = slice(g * 8, (g + 1) * 8)

                    HBd = [PA[1][:, 1024:1536], PA[1][:, 512:1024]]
                    HBKd = [PAK[1][2], PAK[1][1]]
                    Yd = [PA[0][:, 0:512], PM[0][:, 0:512]]
                    YKd = [PAK[0][0], PMK[0]]
                    hdir_bf = [hf_bf, LT2]
                    seg_n = [0]

                    def state_zero_bits(d):
                        mm(HBd[d], zeros_b, cb[:, 0:512], True, True, ["cb"], [HBKd[d]])

                    def state_update(d, c):
                        xw = xwt[d]
                        tt("dve", xw.rearrange("p (h q) -> p h q", h=8), x_tm[:, c, :].rearrange("p (h q) -> p h q", h=8),
                           wst[:, d, c, hs].unsqueeze(2).to_broadcast([128, 8, 64]), ALU.mult, ["x_tm", "wst"], [("xw", d)])
                        tt("dve", HBd[d].rearrange("p (h q) -> p h q", h=8), HBd[d].rearrange("p (h q) -> p h q", h=8),
                           kcd[:, d, c, hs].unsqueeze(2).to_broadcast([128, 8, 64]), ALU.mult, [HBKd[d], "kcd"], [HBKd[d]])
                        mm(HBd[d], B_tm[:, c, :], xw, False, False, ["B_tm", ("xw", d)], [HBKd[d]])

                    def state_out(d, u):
                        cp("act", H[d], HBd[d], [HBKd[d]], [("H", d)])
                        P.dma("sp", st_d[u, j, d][:, g * 512:(g + 1) * 512], H[d], reads=[("H", d)], chan=("sto", d))

                    def diag_pass(c):
                        ck = slice(c * 128, (c + 1) * 128)
                        par = c % 2
                        Yb, YK = Yd[par], YKd[par]
                        first = True
                        for d in range(2):
                            for hq in range(2):
                                nseg = (seg_n[0]) % 2
                                seg_n[0] += 1
                                sb = 1 + nseg
                                seg = PA[0][:, sb * 512:(sb + 1) * 512]
                                sk = PAK[0][sb]
                                lt_ = LT[nseg]
                                mt_ = MT[nseg]
                                jj0 = d * 32 + g * 8 + hq * 4
                                for hh in range(4):
                                    mm(seg[:, hh * 128:(hh + 1) * 128], ecol[:, jj0 + hh:jj0 + hh + 1].to_broadcast([128, 128]),
                                       cshlT[:, ck], start=(hh == 0), stop=False, reads=["cb", "cshlT"], writes=[sk])
                                mm(seg.rearrange("p (h l) -> p h l", h=4), nuhlT[:, ck],
                                   ecol[:, jj0:jj0 + 4].unsqueeze(2).to_broadcast([128, 4, 128]), False, False, ["cb", "nuhlT"], [sk])
                                mm(seg, ident, negm[d], False, True, ["cb"], [sk])
                                act(lt_, seg, AF.Exp, [sk], [("LT", nseg)])
                                tt("dve", mt_.rearrange("p (h l) -> p h l", h=4), lt_.rearrange("p (h l) -> p h l", h=4),
                                   CBT[:, c, :].unsqueeze(1).to_broadcast([128, 4, 128]), ALU.mult,
                                   [("LT", nseg), "CBT"], [("MT", nseg)])
                                for hh in range(4):
                                    hl = hq * 4 + hh
                                    mm(Yb[:, hl * 64:(hl + 1) * 64], mt_[:, hh * 128:(hh + 1) * 128], x_tm[:, c, hl * 64:(hl + 1) * 64],
                                       start=first, stop=False, reads=[("MT", nseg), "x_tm"], writes=[YK])
                                    first = False
                        tt("dve", xD.rearrange("p (h q) -> p h q", h=8), x_tm[:, c, :].rearrange("p (h q) -> p h q", h=8),
                           d_bc[:, hs].unsqueeze(2).to_broadcast([128, 8, 64]), ALU.mult, ["x_tm", "rows"], ["xD"])
                        mm(Yb, ident, xD, False, True, ["cb", "xD"], [YK])
                        cp("act", hb_bf[:, c, :], Yb, [YK], [("stash", c)])

                    def sweep_step(d, c, finalize):
                        ck = slice(c * 128, (c + 1) * 128)
                        if d == 1:
                            if c == NCH - 1:
                                state_zero_bits(1)
                            if c == 7:
                                P.dma("sp", H[1], init_d[j, 1][:, g * 512:(g + 1) * 512], writes=[("H", 1)], chan="ldh1")
                                cp("act", HBd[1], H[1], [("H", 1)], [HBKd[1]])
                        else:
                            if c == 0:
                                state_zero_bits(0)
                                P.dma("sp", H[0], init_d[j, 0][:, g * 512:(g + 1) * 512], writes=[("H", 0)], chan="ldh0")
                                cp("act", HBd[0], H[0], [("H", 0)], [HBKd[0]])
                            if c == 8:
                                memset("dve", HBd[0], 0.0, [HBKd[0]])
                        hbf = hdir_bf[d]
                        hbk = ("hdir_bf", d)
                        cp("act", hbf, HBd[d], [HBKd[d]], [hbk])
                        ob = PA[1][:, 0:512]
                        ok_ = PAK[1][0]
                        mm(ob, CgT[:, ck], hbf, True, True, ["CgT", hbk], [ok_])
                        tt("dve", yo[d].rearrange("p (h q) -> p h q", h=8), ob.rearrange("p (h q) -> p h q", h=8),
                           e_offk[:, d, c, hs].unsqueeze(2).to_broadcast([128, 8, 64]), ALU.mult, [ok_, "e_offk"], [("yo", d)])
                        if finalize:
                            par = c % 2
                            y_tm = y_tm2[par]
                            ytk = ("y_tm", par)
                            Yf = PA[0][:, (1 + par) * 512:(2 + par) * 512]
                            YfK = PAK[0][1 + par]
                            mm(Yf, ident, hb_bf[:, c, :], True, False, ["cb", ("stash", c)], [YfK])
                            mm(Yf, ident, yo[d], False, True, ["cb", ("yo", d)], [YfK])
                            cp("act", y_tm, Yf, [YfK], [ytk])
                            for t_ in range(4):
                                tr(PM[1][:, t_ * 128:(t_ + 1) * 128], y_tm[:, t_ * 128:(t_ + 1) * 128], [ytk], [PMK[1]], t_ == 0)
                            ygv = ygT[:, g * 4:(g + 1) * 4, ck]
                            gkeys = [("ygT", g * 4 + t_) for t_ in range(4)]
                            tt("dve", ygv, PM[1][:, 0:512].rearrange("p (t l) -> p t l", t=4), ygv, ALU.mult, [PMK[1]] + gkeys, gkeys)
                        else:
                            tt("dve", hb_bf[:, c, :], hb_bf[:, c, :], yo[d], ALU.add, [("stash", c), ("yo", d)], [("stash", c)])
                        state_update(d, c)
                        if d == 1 and c % 2 == 0:
                            state_out(1, c // 2)
                        if d == 0 and c % 2 == 1:
                            state_out(0, (c - 1) // 2)

                    for stp in range(NCH // 2):
                        diag_pass(stp)
                        diag_pass(NCH - 1 - stp)
                        sweep_step(1, NCH - 1 - stp, finalize=False)
                        sweep_step(0, stp, finalize=False)
                    for stp in range(NCH // 2, NCH):
                        sweep_step(1, NCH - 1 - stp, finalize=True)
                        sweep_step(0, stp, finalize=True)

                P.fill = None
                P.barrier()
                sq2 = [xraw.bitcast(BF16)[:, 0:T], cacc.bitcast(BF16)[:, 0:T]]
                rms_stats(lambda k: ygT[:, k, :], 16, sq2, 1.0 / 2048.0, lambda k: ("ygT", k))
                for k in range(16):
                    tsc("dve", ygT[:, k, :], ygT[:, k, :], normw[:, k:k + 1], ALU.mult, [("ygT", k), "sm"], [("ygT", k)])
                tmpo = [xraw, cacc]
                for ot in range(8):
                    a = ot % 2
                    wv, wk = ws.get(w_out[j][:, ot * 128:(ot + 1) * 128], 16, 128)
                    lin_tile(a, wv, wk, 0, lambda k: ygT[:, k, :], lambda k: ("ygT", k), [(k, k) for k in range(16)], True, True)
                    tt("dve", tmpo[a], PA[a][:, 0:T], rstd, ALU.mult, PAK[a] + ["rstd"], [("tmpo", a)])
                    resid_add(ot, tmpo[a], [("tmpo", a)], mb, mk, 16)

            a0, b0 = ada_steps(0)
            for s in a0:
                s()
            for i in range(n_layers):
                mb, mkA, mkB = mods[i % 2], ("modsA", i % 2), ("modsB", i % 2)
                if i % 2 == 0:
                    ssd(i // 2, i, mb, mkA, b0 if i == 0 else ())
                else:
                    fnet(i // 2, i, mb, mkA)
                inter = []
                if i + 1 < 4:
                    an, bn = ada_steps(i + 1)
                    inter = an + bn
                mlp(i, mb, mkB, inter)
            P.barrier()
            sqf = [view_at(PR0 + q * T * 2, T * 2, BF16) for q in range(2)]
            outf = [view_at(PR0 + 2 * T * 2 + q * T * 4, T * 4, F32) for q in range(2)]
            rms_stats(lambda k: x[:, k, :], 8, sqf, 1.0 / 1024.0, lambda k: ("x", k))
            for k in range(8):
                stt(outf[k % 2], x[:, k, :], sm[:, SM_FIN + k:SM_FIN + k + 1], rstd, ALU.mult, ALU.mult,
                    [("x", k), "sm", "rstd"], [("outf", k % 2)])
                P.dma("sp", yT_d[k * 128:(k + 1) * 128, :], outf[k % 2], reads=[("outf", k % 2)], chan=("out", k % 2))

        Pd = Prog(nc, dry=True)
        wsd = WStream(Pd, slots)
        gen(Pd, wsd)
        P = Prog(nc)
        ws = WStream(P, slots, plan=wsd.rec)
        gen(P, ws)
        assert ws.n == len(wsd.rec)
        P.emit()
        build_program.stats = dict(P.stats, est_us=getattr(P, "est_us", None), n_fill=P.n_fill)
    return nc


def _bf(a):
    return np.asarray(a, dtype=np.float32).astype(ml_dtypes.bfloat16)


def _consts():
    cbm = np.zeros((128, NCB), np.float32)
    cbm[:, CB_ID:CB_ID + 128] = np.eye(128)
    cbm[:, CB_ONES:CB_ONES + 128] = 1.0
    s = np.arange(128)[:, None]
    l = np.arange(128)[None, :]
    cbm[:, CB_TRIF:CB_TRIF + 128] = (s <= l)
    cbm[:, CB_TRIB:CB_TRIB + 128] = (s >= l)
    nf = np.where(s <= l, 0.0, -30000.0)
    nb = np.where(s >= l, 0.0, -30000.0)
    cbm[:, CB_NEGM:CB_NEGM + 512] = np.tile(nf, (1, 4))
    cbm[:, CB_NEGM + 512:CB_NEGM + 1024] = np.tile(nb, (1, 4))
    for jj in range(64):
        cbm[jj, CB_ECOL + jj] = 1.0
        cbm[64 + jj, CB_ECOL + jj] = 1.0
    d = np.arange(256)
    ang = 2.0 * np.pi * np.outer(d, d) / 256.0
    csd = np.concatenate([np.cos(ang), np.sin(ang)], axis=1)

    def dft(L):
        k = np.arange(L)
        a = 2.0 * np.pi * np.outer(k, k) / L
        sc = 1.0 / np.sqrt(L * 256.0)
        return np.stack([np.cos(a) * sc, -np.sin(a) * sc], axis=1)

    d256 = dft(256)
    d1024 = dft(1024)
    blk = np.zeros((1024, 2, 1024))
    for u in range(4):
        blk[u * 256:(u + 1) * 256, :, u * 256:(u + 1) * 256] = d256
    return _bf(cbm), _bf(csd), _bf(d1024), _bf(blk), _bf(d256)


def _col(v, ntiles):
    return np.asarray(v, np.float32).reshape(ntiles, 128).T


_NC_CACHE = {}


def kernel(x_prompt, x_sample, state_ssd, c, c_ctx, ada_w, ada_b, norm_mix_w, norm_mlp_w,
           ssd_w_in, ssd_conv_w, ssd_conv_b, ssd_dt_bias, ssd_a_log, ssd_d, ssd_norm_w,
           ssd_w_out, fno_w_out, fno_b_out, mlp_w1, mlp_w2, final_norm_w):
    f32 = lambda a: np.ascontiguousarray(np.asarray(a, dtype=np.float32))
    x_prompt, x_sample, state_ssd = f32(x_prompt), f32(x_sample), f32(state_ssd)
    c, c_ctx = f32(c), f32(c_ctx)
    cbm, csd, dft_dense, dft_blk, d256 = _consts()

    smalls = np.zeros((128, NSM), np.float32)
    for i in range(4):
        b = i * SM_LAYER
        smalls[:, b:b + 48] = _col(f32(ada_b)[i], 48)
        smalls[:, b + 48:b + 56] = _col(f32(norm_mix_w)[i], 8)
        smalls[:, b + 56:b + 64] = _col(f32(norm_mlp_w)[i], 8)
    for j in range(2):
        b = SM_SSD0 + j * 128
        cw = f32(ssd_conv_w)[j]
        cwt = np.stack([_col(cw[k], 24) for k in range(3)], axis=2)
        smalls[:, b:b + 72] = cwt.reshape(128, 72)
        smalls[:, b + 72:b + 96] = _col(f32(ssd_conv_b)[j], 24)
        smalls[:, b + 96:b + 112] = _col(f32(ssd_norm_w)[j], 16)
        smalls[:, SM_FNO0 + j * 8:SM_FNO0 + j * 8 + 8] = _col(f32(fno_b_out)[j], 8)
    smalls[:, SM_FIN:SM_FIN + 8] = _col(f32(final_norm_w), 8)

    weights = dict(ada_w=f32(ada_w), ssd_w_in=f32(ssd_w_in), ssd_w_out=f32(ssd_w_out), fno_w_out=f32(fno_w_out),
                   mlp_w1=f32(mlp_w1), mlp_w2=f32(mlp_w2))
    in_maps = []
    for core in range(8):
        rows = np.zeros((1, NR), np.float32)
        for j in range(2):
            rows[0, j * RW_SSD:j * RW_SSD + 64] = f32(ssd_dt_bias)[j].reshape(64)
            rows[0, j * RW_SSD + 64:j * RW_SSD + 128] = f32(ssd_a_log)[j].reshape(64)
            rows[0, j * RW_SSD + 128:j * RW_SSD + 160] = f32(ssd_d)[j]
        kf = np.zeros((2, 10), np.float32)
        ncut = np.zeros(19, np.float32)
        if core < 2:
            toks = np.concatenate([x_sample[core], x_prompt[30 + core]], axis=0)
            condA, condB = c[core], c_ctx
            init = np.stack([np.stack([state_ssd[core, j, d].transpose(2, 0, 1).reshape(128, 2048) for d in range(2)])
                             for j in range(2)])
            kf[0] = [1, 1, 1, 1, 1, 1, 1, 1, 1, 1]
            kf[1] = [1, 1, 1, 1, 1, 1, 1, 1, 1, 1]
            ncut[0:16] = -1.0
            dftA = dft_dense
        else:
            p0 = 5 * (core - 2)
            toks = np.concatenate([x_prompt[p0 + u] for u in range(5)], axis=0)
            condA, condB = c_ctx, c_ctx
            init = np.zeros((2, 2, 128, 2048), np.float32)
            kf[0] = [1, 1, 0, 1, 0, 1, 0, 1, 1, 1]
            kf[1] = [1, 0, 1, 0, 1, 0, 1, 1, 1, 1]
            for jb in (4, 8, 12, 16):
                ncut[jb - 1] = -1.0
            dftA = dft_blk
        rows[0, RW_KF:RW_KF + 20] = kf.reshape(20)
        rows[0, RW_CUT:RW_CUT + 19] = ncut
        cond = np.stack([_col(condA, 8), _col(condB, 8)], axis=2).reshape(128, 16)
        m = dict(xT=np.ascontiguousarray(toks.T), condT=np.ascontiguousarray(cond), smalls=smalls, rows=rows,
                 init_st=np.ascontiguousarray(init), cb=cbm, dftA=dftA, dftB=d256, csd=csd)
        m.update(weights)
        in_maps.append(m)

    if "nc" not in _NC_CACHE:
        _NC_CACHE["nc"] = build_program()
    nc = _NC_CACHE["nc"]
    res = run_bass_kernel_spmd(nc, in_maps, core_ids=list(range(8)))

    y_prompt = np.zeros((32, 256, 1024), np.float32)
    y_sample = np.zeros((2, 1024, 1024), np.float32)
    new_state = np.zeros((32, 2, 2, 32, 64, 128), np.float32)
    for core in range(8):
        r = res.results[core]
        y = np.asarray(r["yT"], np.float32).T
        so = np.asarray(r["st_out"], np.float32)

        def put_state(p, u):
            for j in range(2):
                for d in range(2):
                    new_state[p, j, d] = so[u, j, d].reshape(128, 32, 64).transpose(1, 2, 0)

        if core < 2:
            y_sample[core] = y[0:1024]
            y_prompt[30 + core] = y[1024:1280]
            put_state(30 + core, 4)
        else:
            p0 = 5 * (core - 2)
            for u in range(5):
                y_prompt[p0 + u] = y[u * 256:(u + 1) * 256]
                put_state(p0 + u, u)
    return (y_prompt, y_sample, new_state)
```
